# Optimizing a Trainium2 kernel written in Bass

```python
import math
import jax, jax.numpy as jnp
from jax import lax
import numpy as np

D_MODEL = 1024
BATCH = 8
SEQ = 2048
DEPTH = 1
DEC_BATCH = 128
DEC_SEQ = 1
PAST_LEN = 16384
PAGE_SIZE = 128

D_MIX = 2 * D_MODEL
D_SSD = D_MIX // 2
D_SC = D_MIX - D_SSD
SSD_HEAD_DIM = 64
SSD_HEADS = D_SSD // SSD_HEAD_DIM
SSD_GROUPS = 2
SSD_STATE = 128
SSD_CONV = 4
SSD_CHUNK = 128
SSD_CONV_DIM = D_SSD + 2 * SSD_GROUPS * SSD_STATE
SC_GROUPS = 16
SC_CONV = 3
D_FF = ((8 * D_MODEL // 3 + 127) // 128) * 128
FFN_CONV = 3
EPS = 1e-5
SPLITS = [D_SSD, D_SSD + SSD_CONV_DIM, D_SSD + SSD_CONV_DIM + SSD_HEADS,
          D_SSD + SSD_CONV_DIM + SSD_HEADS + D_SC,
          D_SSD + SSD_CONV_DIM + SSD_HEADS + 2 * D_SC]
D_IN_PROJ = D_SSD + SSD_CONV_DIM + SSD_HEADS + 3 * D_SC

kernel_name = "hybrid_ssd_shortconv_convffn_step"


def rms_norm(x, w):
    xf = x.astype(jnp.float32)
    y = xf * lax.rsqrt(jnp.mean(xf * xf, axis=-1, keepdims=True) + EPS)
    return (y * w.astype(jnp.float32)).astype(x.dtype)


def causal_dwconv(x, buf, w, b=None):
    K = w.shape[0]
    L = x.shape[1]
    xp = jnp.concatenate([buf.astype(x.dtype), x], axis=1)
    y = xp[:, 0:L] * w[0]
    for k in range(1, K):
        y = y + xp[:, k:k + L] * w[k]
    if b is not None:
        y = y + b
    return y, xp[:, L:]


def ssd_scan(x, dt, a, b_in, c_in, h0):
    bsz, l, nh, p = x.shape
    g = b_in.shape[2]
    r = nh // g
    n = b_in.shape[-1]
    q = SSD_CHUNK if l % SSD_CHUNK == 0 else l
    nc = l // q
    xd = (x * dt[..., None]).reshape(bsz, nc, q, g, r, p)
    da = (dt * a).reshape(bsz, nc, q, g, r)
    bc = b_in.reshape(bsz, nc, q, g, n)
    cc = c_in.reshape(bsz, nc, q, g, n)
    a_cum = jnp.cumsum(da, axis=2)
    mask = jnp.tril(jnp.ones((q, q), dtype=bool))
    seg = a_cum[:, :, :, None] - a_cum[:, :, None, :]
    decay = jnp.exp(jnp.where(mask[:, :, None, None], seg, -jnp.inf))
    cb = jnp.einsum('bclgn,bcsgn->bclsg', cc, bc)
    y_diag = jnp.einsum('bclsg,bclsgr,bcsgrp->bclgrp', cb, decay, xd)
    decay_s = jnp.exp(a_cum[:, :, -1:] - a_cum)
    states = jnp.einsum('bcsgn,bcsgr,bcsgrp->bcgrpn', bc, decay_s, xd)
    chunk_decay = jnp.exp(a_cum[:, :, -1])

    def step(h, inp):
        st, dec = inp
        return dec[..., None, None] * h + st, h

    h_last, h_prev = lax.scan(step, h0.reshape(bsz, g, r, p, n),
                              (jnp.swapaxes(states, 0, 1), jnp.swapaxes(chunk_decay, 0, 1)))
    h_prev = jnp.swapaxes(h_prev, 0, 1)
    y_off = jnp.einsum('bclgn,bcgrpn,bclgr->bclgrp', cc, h_prev, jnp.exp(a_cum))
    y = (y_diag + y_off).reshape(bsz, l, nh, p)
    return y, h_last.reshape(bsz, nh, p, n)


def hybrid_mixer(u, st_ssm, st_ssd_conv, st_sc_conv, w_in, ssd_conv_w, ssd_conv_b,
                 ssd_dt_bias, ssd_a_log, ssd_d, ssd_norm_w, sc_conv_w, w_out):
    bsz, l, _ = u.shape
    proj = u @ w_in
    z, xbc, dt_raw, g_b, g_c, h = jnp.split(proj, SPLITS, axis=-1)
    xbc, new_ssd_conv = causal_dwconv(xbc, st_ssd_conv, ssd_conv_w, ssd_conv_b)
    xbc = jax.nn.silu(xbc).astype(jnp.float32)
    xs, bs, cs = jnp.split(xbc, [D_SSD, D_SSD + SSD_GROUPS * SSD_STATE], axis=-1)
    xs = xs.reshape(bsz, l, SSD_HEADS, SSD_HEAD_DIM)
    dt = jax.nn.softplus(dt_raw.astype(jnp.float32) + ssd_dt_bias.astype(jnp.float32))
    a = -jnp.exp(ssd_a_log.astype(jnp.float32))
    y, new_ssm = ssd_scan(xs, dt, a,
                          bs.reshape(bsz, l, SSD_GROUPS, SSD_STATE),
                          cs.reshape(bsz, l, SSD_GROUPS, SSD_STATE),
                          st_ssm.astype(jnp.float32))
    y = y + ssd_d.astype(jnp.float32)[:, None] * xs
    y = y.reshape(bsz, l, D_SSD) * jax.nn.silu(z.astype(jnp.float32))
    yg = y.reshape(bsz, l, SSD_GROUPS, D_SSD // SSD_GROUPS)
    yg = yg * lax.rsqrt(jnp.mean(yg * yg, axis=-1, keepdims=True) + EPS)
    y_ssd = (yg.reshape(bsz, l, D_SSD) * ssd_norm_w.astype(jnp.float32)).astype(u.dtype)
    sc, new_sc_conv = causal_dwconv(g_c * h, st_sc_conv, sc_conv_w)
    y_sc = g_b * sc
    out = jnp.concatenate([y_ssd, y_sc], axis=-1) @ w_out
    return out, new_ssm.astype(u.dtype), new_ssd_conv, new_sc_conv


def conv_ffn(u, st_ffn, w_ffn_in, ffn_conv_w, ffn_conv_b, w_down):
    gate, up = jnp.split(u @ w_ffn_in, [D_FF], axis=-1)
    gate, new_st = causal_dwconv(gate, st_ffn, ffn_conv_w, ffn_conv_b)
    return (jax.nn.silu(gate) * up) @ w_down, new_st


def run_trunk(x, st_ssm, st_ssd_conv, st_sc_conv, st_ffn, norm_mix_w, w_in, ssd_conv_w,
              ssd_conv_b, ssd_dt_bias, ssd_a_log, ssd_d, ssd_norm_w, sc_conv_w, w_out,
              norm_ffn_w, w_ffn_in, ffn_conv_w, ffn_conv_b, w_down, norm_final_w):
    n_ssm, n_ssd_conv, n_sc_conv, n_ffn = [], [], [], []
    for i in range(DEPTH):
        m, s1, s2, s3 = hybrid_mixer(rms_norm(x, norm_mix_w[i]), st_ssm[i], st_ssd_conv[i],
                                     st_sc_conv[i], w_in[i], ssd_conv_w[i], ssd_conv_b[i],
                                     ssd_dt_bias[i], ssd_a_log[i], ssd_d[i], ssd_norm_w[i],
                                     sc_conv_w[i], w_out[i])
        x = x + m
        f, s4 = conv_ffn(rms_norm(x, norm_ffn_w[i]), st_ffn[i], w_ffn_in[i],
                         ffn_conv_w[i], ffn_conv_b[i], w_down[i])
        x = x + f
        n_ssm.append(s1); n_ssd_conv.append(s2); n_sc_conv.append(s3); n_ffn.append(s4)
    y = rms_norm(x, norm_final_w)
    return (y, jnp.stack(n_ssm), jnp.stack(n_ssd_conv), jnp.stack(n_sc_conv), jnp.stack(n_ffn))


def setup_inputs(seed: int = 0) -> dict:
    key = jax.random.key(seed)
    ks = jax.random.split(key, 24)
    f32 = jnp.float32
    nrm = lambda k, shape, s: jax.random.normal(k, shape, f32) * s
    dt0 = jnp.exp(jax.random.uniform(ks[10], (DEPTH, SSD_HEADS), f32,
                                     math.log(1e-3), math.log(1e-1)))
    return {
        "x_prompt": nrm(ks[0], (BATCH, SEQ, D_MODEL), 1.0),
        "x_sample": nrm(ks[1], (DEC_BATCH, DEC_SEQ, D_MODEL), 1.0),
        "state_ssm": nrm(ks[2], (DEPTH, DEC_BATCH, SSD_HEADS, SSD_HEAD_DIM, SSD_STATE), 0.1),
        "state_ssd_conv": nrm(ks[3], (DEPTH, DEC_BATCH, SSD_CONV - 1, SSD_CONV_DIM), 1.0),
        "state_short_conv": nrm(ks[4], (DEPTH, DEC_BATCH, SC_CONV - 1, D_SC), 1.0),
        "state_ffn_conv": nrm(ks[5], (DEPTH, DEC_BATCH, FFN_CONV - 1, D_FF), 1.0),
        "norm_mix_w": 1.0 + nrm(ks[6], (DEPTH, D_MODEL), 0.02),
        "w_in": nrm(ks[7], (DEPTH, D_MODEL, D_IN_PROJ), D_MODEL ** -0.5),
        "ssd_conv_w": nrm(ks[8], (DEPTH, SSD_CONV, SSD_CONV_DIM), SSD_CONV ** -0.5),
        "ssd_conv_b": nrm(ks[9], (DEPTH, SSD_CONV_DIM), 0.02),
        "ssd_dt_bias": dt0 + jnp.log(-jnp.expm1(-dt0)),
        "ssd_a_log": jnp.log(jax.random.uniform(ks[11], (DEPTH, SSD_HEADS), f32, 1.0, 16.0)),
        "ssd_d": 1.0 + nrm(ks[12], (DEPTH, SSD_HEADS), 0.1),
        "ssd_norm_w": 1.0 + nrm(ks[13], (DEPTH, D_SSD), 0.02),
        "sc_conv_w": nrm(ks[14], (DEPTH, SC_CONV, D_SC), SC_CONV ** -0.5),
        "w_out": nrm(ks[15], (DEPTH, D_MIX, D_MODEL), D_MIX ** -0.5),
        "norm_ffn_w": 1.0 + nrm(ks[16], (DEPTH, D_MODEL), 0.02),
        "w_ffn_in": nrm(ks[17], (DEPTH, D_MODEL, 2 * D_FF), D_MODEL ** -0.5),
        "ffn_conv_w": nrm(ks[18], (DEPTH, FFN_CONV, D_FF), FFN_CONV ** -0.5),
        "ffn_conv_b": nrm(ks[19], (DEPTH, D_FF), 0.02),
        "w_down": nrm(ks[20], (DEPTH, D_FF, D_MODEL), D_FF ** -0.5),
        "norm_final_w": 1.0 + nrm(ks[21], (D_MODEL,), 0.02),
    }


def reference(x_prompt, x_sample, state_ssm, state_ssd_conv, state_short_conv, state_ffn_conv,
              norm_mix_w, w_in, ssd_conv_w, ssd_conv_b, ssd_dt_bias, ssd_a_log, ssd_d,
              ssd_norm_w, sc_conv_w, w_out, norm_ffn_w, w_ffn_in, ffn_conv_w, ffn_conv_b,
              w_down, norm_final_w):
    bp = x_prompt.shape[0]
    dtp = x_prompt.dtype
    z_ssm = jnp.zeros((DEPTH, bp, SSD_HEADS, SSD_HEAD_DIM, SSD_STATE), dtp)
    z_ssd_conv = jnp.zeros((DEPTH, bp, SSD_CONV - 1, SSD_CONV_DIM), dtp)
    z_sc_conv = jnp.zeros((DEPTH, bp, SC_CONV - 1, D_SC), dtp)
    z_ffn = jnp.zeros((DEPTH, bp, FFN_CONV - 1, D_FF), dtp)
    y_prompt, p_ssm, p_ssd_conv, p_sc_conv, p_ffn = run_trunk(
        x_prompt, z_ssm, z_ssd_conv, z_sc_conv, z_ffn, norm_mix_w, w_in, ssd_conv_w,
        ssd_conv_b, ssd_dt_bias, ssd_a_log, ssd_d, ssd_norm_w, sc_conv_w, w_out,
        norm_ffn_w, w_ffn_in, ffn_conv_w, ffn_conv_b, w_down, norm_final_w)
    y_sample, s_ssm, s_ssd_conv, s_sc_conv, s_ffn = run_trunk(
        x_sample, state_ssm, state_ssd_conv, state_short_conv, state_ffn_conv, norm_mix_w,
        w_in, ssd_conv_w, ssd_conv_b, ssd_dt_bias, ssd_a_log, ssd_d, ssd_norm_w, sc_conv_w,
        w_out, norm_ffn_w, w_ffn_in, ffn_conv_w, ffn_conv_b, w_down, norm_final_w)
    return (y_prompt, y_sample, p_ssm, p_ssd_conv, p_sc_conv, p_ffn,
            s_ssm, s_ssd_conv, s_sc_conv, s_ffn)
```

```python
import numpy as np
from contextlib import ExitStack
import concourse.bass as bass
import concourse.mybir as mybir
from concourse.bass_utils import run_bass_kernel_spmd

F32 = mybir.dt.float32
BF16 = mybir.dt.bfloat16
ALU = mybir.AluOpType
AF = mybir.ActivationFunctionType
AX = mybir.AxisListType

T = 2048
NCH = 16
NS = 16
TT = T + NS
EPS = 1e-5
DFF = 2816
NFT = 22

O_Z, O_XBC, O_DT, O_GB, O_GC, O_H = 0, 1024, 2560, 2576, 3600, 4624


class Buf:
    def __init__(self, t):
        self.t = t
        self.w = {}
        self.r = {}


class Alias:
    def __init__(self, parent, view):
        self.p = parent
        self.t = view

    @property
    def w(self):
        return self.p.w

    @w.setter
    def w(self, v):
        self.p.w = v

    @property
    def r(self):
        return self.p.r

    @r.setter
    def r(self, v):
        self.p.r = v


class Prog:
    def __init__(self, nc, es):
        self.nc = nc
        self.es = es
        self.eng = {"pe": nc.tensor, "act": nc.scalar, "dve": nc.vector, "pool": nc.gpsimd, "sp": nc.sync}
        self.semh = {}
        self.cnt = {}
        for k in ["pe", "act", "dve", "pool"]:
            self.semh[k] = es.enter_context(nc.semaphore("s_" + k))
            self.cnt[k] = 0
        self.seen = {k: {} for k in self.eng}
        self.nd = 0

    def dsem(self, name=None):
        self.nd += 1
        key = "d%d" % self.nd
        self.semh[key] = self.es.enter_context(self.nc.semaphore("s_" + key))
        self.cnt[key] = 0
        return key

    def _deps(self, e, R, W):
        deps = {}

        def add(d, own_ok):
            for k, v in d.items():
                if k == e and not own_ok:
                    continue
                if deps.get(k, 0) < v:
                    deps[k] = v

        for b in R:
            add(b.w, True)
        for b in W:
            add(b.w, True)
            add(b.r, True)
        if e == "pe":
            deps.pop("pe", None)
        return deps

    def _wait(self, e, deps):
        for k, v in deps.items():
            if self.seen[e].get(k, 0) >= v:
                continue
            self.eng[e].wait_ge(self.semh[k], v)
            self.seen[e][k] = v

    def _commit(self, key, val, R, W):
        for b in R:
            if b.r.get(key, 0) < val:
                b.r[key] = val
        for b in W:
            b.w = {key: val}
            b.r = {}

    def op(self, e, fn, R=(), W=()):
        self._wait(e, self._deps(e, R, W))
        ins = fn(self.eng[e])
        self.cnt[e] += 1
        ins.then_inc(self.semh[e], 1)
        self._commit(e, self.cnt[e], R, W)

    def dma(self, q, pairs, key, R=(), W=(), **kw):
        self._wait(q, self._deps(q, R, W))
        for (o, i) in pairs:
            self.eng[q].dma_start(out=o, in_=i, **kw).then_inc(self.semh[key], 16)
            self.cnt[key] += 16
        self._commit(key, self.cnt[key], R, W)

    def barrier(self):
        allv = dict(self.cnt)
        for e in self.eng:
            self._wait(e, {k: v for k, v in allv.items() if v > 0})

    def finish(self):
        self._wait("sp", {k: v for k, v in self.cnt.items() if v > 0})


def build(stop_after=99, taps=()):
    nc = bass.Bass("TRN2", target_bir_lowering=False)

    def din(name, shape, dt=F32):
        return nc.dram_tensor(name, list(shape), dt, kind="ExternalInput").ap()

    def dout(name, shape, dt=F32):
        return nc.dram_tensor(name, list(shape), dt, kind="ExternalOutput").ap()

    xp = din("xp", [T, 1024])
    xs = din("xs", [NS, 1024])
    st_ssm = din("st_ssm", [128, 16384])
    st_xbc_f = din("st_xbc_f", [128, 12, 4, NS])
    st_xbc_n = din("st_xbc_n", [NS, 3, 1536])
    st_sc_f = din("st_sc_f", [128, 8, 3, NS])
    st_sc_n = din("st_sc_n", [NS, 2, 1024])
    st_ffn_f = din("st_ffn_f", [128, NFT, 3, NS])
    st_ffn_n = din("st_ffn_n", [NS, 2, DFF])
    w_in_t = din("w_in_t", [36, 128, 8, 128])
    w_dt = din("w_dt", [128, 8, 16])
    w_z = din("w_z", [128, 8, 1024])
    w_out_t = din("w_out_t", [128, 16, 1024])
    w_ffn_t = din("w_ffn_t", [2 * NFT, 128, 8, 128])
    w_down_t = din("w_down_t", [128, NFT, 1024])
    nmix_bc = din("nmix_bc", [128, 1024])
    nffn_bc = din("nffn_bc", [128, 1024])
    nfin_bc = din("nfin_bc", [128, 1024])
    nssd_bc = din("nssd_bc", [128, 1024])
    d_bc = din("d_bc", [128, 1024])
    alog_bc = din("alog_bc", [128, 16])
    dtb = din("dtb", [128, 16])
    cw_xbc = din("cw_xbc", [128, 12, 4])
    cb_xbc = din("cb_xbc", [128, 12])
    cw_sc = din("cw_sc", [128, 8, 3])
    cw_ffn = din("cw_ffn", [128, NFT, 3])
    cb_ffn = din("cb_ffn", [128, NFT])

    y_p = dout("y_p", [T, 1024])
    y_s = dout("y_s", [NS, 1024])
    ssm_p = dout("ssm_p", [1024, 128])
    xbc_p = dout("xbc_p", [3, 1536])
    sc_p = dout("sc_p", [2, 1024])
    ffn_p = dout("ffn_p", [2, DFF])
    ssm_s = dout("ssm_s", [128, 16384])
    xbc_s = dout("xbc_s", [NS, 3, 1536])
    sc_s = dout("sc_s", [NS, 2, 1024])
    ffn_s = dout("ffn_s", [NS, 2, DFF])
    tapo = {}
    for (nm, shp, dt) in taps:
        tapo[nm] = dout("tap_" + nm, shp, dt)

    x1_d = nc.dram_tensor("x1_d", [TT, 1024], F32, kind="Internal").ap()
    wo_bf = nc.dram_tensor("wo_bf", [128, 16, 1024], BF16, kind="Internal").ap()
    wd_bf = nc.dram_tensor("wd_bf", [128, NFT, 1024], BF16, kind="Internal").ap()
    scr_d = nc.dram_tensor("scr_d", [NS, 16 + 1024 + 1024 + 1024], F32, kind="Internal").ap()
    ys_d = nc.dram_tensor("ys_d", [128, 128], F32, kind="Internal").ap()

    with ExitStack() as es:
        P = Prog(nc, es)

        def sb(name, shape, dt, stack=es):
            return Buf(stack.enter_context(nc.sbuf_tensor(name, list(shape), dt)))

        def ps(name, shape, dt, stack):
            return Buf(stack.enter_context(nc.psum_tensor(name, list(shape), dt)))

        d_const = P.dsem()
        d_out = P.dsem()
        d_pass = P.dsem()

        ident_f = sb("ident_f", [128, 128], F32)
        ident_b = sb("ident_b", [128, 128], BF16)
        tri_f = sb("tri_f", [128, 128], F32)
        lst_f = sb("lst_f", [128, 128], F32)
        ones_f = sb("ones_f", [128, 128], F32)
        mhalf = sb("mhalf", [128, 4], F32)
        P.op("pool", lambda e: e.memset(ident_f.t[:], 1.0), W=[ident_f])
        P.op("pool", lambda e: e.affine_select(out=ident_f.t[:], in_=ident_f.t[:], pattern=[[-1, 128]],
                                               compare_op=ALU.is_equal, fill=0.0, base=0, channel_multiplier=1),
             R=[ident_f], W=[ident_f])
        P.op("pool", lambda e: e.memset(tri_f.t[:], 1.0), W=[tri_f])
        P.op("pool", lambda e: e.affine_select(out=tri_f.t[:], in_=tri_f.t[:], pattern=[[1, 128]],
                                               compare_op=ALU.is_ge, fill=0.0, base=0, channel_multiplier=-1),
             R=[tri_f], W=[tri_f])
        P.op("pool", lambda e: e.memset(lst_f.t[:], 1.0), W=[lst_f])
        P.op("pool", lambda e: e.affine_select(out=lst_f.t[:], in_=lst_f.t[:], pattern=[[-1, 128]],
                                               compare_op=ALU.is_gt, fill=0.0, base=0, channel_multiplier=1),
             R=[lst_f], W=[lst_f])
        P.op("pool", lambda e: e.memset(ones_f.t[:], 1.0), W=[ones_f])
        P.op("pool", lambda e: e.memset(mhalf.t[:], -0.5), W=[mhalf])
        P.op("dve", lambda e: e.tensor_copy(ident_b.t[:], ident_f.t[:]), R=[ident_f], W=[ident_b])

        c_nssd = sb("c_nssd", [128, 1024], F32)
        c_d = sb("c_d", [128, 1024], F32)
        c_alog = sb("c_alog", [128, 16], F32)
        c_a = sb("c_a", [128, 16], F32)
        c_dtb = sb("c_dtb", [128, 16], F32)
        c_cwx = sb("c_cwx", [128, 12, 4], F32)
        c_cbx = sb("c_cbx", [128, 12], F32)
        c_cws = sb("c_cws", [128, 8, 3], F32)
        c_cwf = sb("c_cwf", [128, NFT, 3], F32)
        c_cbf = sb("c_cbf", [128, NFT], F32)
        consts = [c_nssd, c_d, c_alog, c_dtb, c_cwx, c_cbx, c_cws, c_cwf, c_cbf]
        P.dma("act", [(c_nssd.t[:], nssd_bc), (c_d.t[:], d_bc), (c_alog.t[:], alog_bc),
                     (c_dtb.t[:], dtb), (c_cwx.t[:], cw_xbc), (c_cbx.t[:], cb_xbc), (c_cws.t[:], cw_sc),
                     (c_cwf.t[:], cw_ffn), (c_cbf.t[:], cb_ffn)], d_const, W=consts)

        raws_x = sb("raws_x", [128, 12, 4, NS], F32)
        raws_s = sb("raws_s", [128, 8, 3, NS], F32)
        raws_f = sb("raws_f", [128, NFT, 3, NS], F32)
        d_raws = P.dsem()
        d_cn = P.dsem()
        P.dma("act", [(raws_x.t[:], st_xbc_f), (raws_s.t[:], st_sc_f), (raws_f.t[:], st_ffn_f)], d_raws,
              W=[raws_x, raws_s, raws_f])
        P.dma("act", [(xbc_s[:, 0:2, :], st_xbc_n[:, 1:3, :]), (sc_s[:, 0:1, :], st_sc_n[:, 1:2, :]),
                     (ffn_s[:, 0:1, :], st_ffn_n[:, 1:2, :])], d_pass)

        wo_bf_b = Buf(wo_bf)
        wd_bf_b = Buf(wd_bf)
        d_pc = P.dsem()

        P.op("act", lambda e: e.activation(out=c_a.t[:], in_=c_alog.t[:], func=AF.Exp), R=[c_alog], W=[c_a])
        P.op("dve", lambda e: e.tensor_scalar(out=c_a.t[:], in0=c_a.t[:], scalar1=-1.0, scalar2=None, op0=ALU.mult),
             R=[c_a], W=[c_a])

        S1 = es.enter_context(ExitStack())
        ycat_ssd = sb("ycat_ssd", [128, 8, TT], BF16, S1)
        us_tok = sb("us_tok", [NS, 1024], BF16, S1)
        xs_T = sb("xs_T", [128, 12, NS], F32, S1)
        dt_s = sb("dt_s", [NS, 16], F32, S1)
        S2 = es.enter_context(ExitStack())
        xnT = S2.enter_context(nc.sbuf_tensor("xnT", [128, 8, TT], BF16, side="right"))
        xnT_b = [Buf(xnT) for _ in range(5)]

        def run_pipeline(n_items, stages):
            st2 = [(st if isinstance(st, tuple) else (st, k)) for k, st in enumerate(stages)]
            maxlag = max(l for _, l in st2)
            for it in range(n_items + maxlag):
                for fn, lag in st2:
                    m = it - lag
                    if 0 <= m < n_items:
                        fn(m)

        def blk_of_tile(i):
            return 4 if i == NCH else i // 4

        def rms_scale_transpose(i, xt, nrows, wbc, ss, rstd, xnb, pT, dstT, dst_bufs, act_out=None, do_tr=True):
            n = nrows
            P.op("act", lambda e: e.activation(out=xnb.t[0:n, :], in_=xt.t[0:n, :], func=AF.Square,
                                               accum_out=ss.t[0:n, 0:1]), R=[xt], W=[xnb, ss])
            P.op("pool", lambda e: e.tensor_scalar(out=rstd.t[0:n, 0:1], in0=ss.t[0:n, 0:1], scalar1=1.0 / 1024,
                                                   scalar2=EPS, op0=ALU.mult, op1=ALU.add), R=[ss], W=[rstd])
            P.op("pool", lambda e: e.tensor_tensor(out=rstd.t[0:n, 0:1], in0=rstd.t[0:n, 0:1], in1=mhalf.t[0:n, 0:1],
                                                   op=ALU.pow), R=[rstd, mhalf], W=[rstd])
            if act_out is not None:
                P.op("dve", lambda e: e.scalar_tensor_tensor(out=act_out.t[0:n, :], in0=xt.t[0:n, :],
                                                             scalar=rstd.t[0:n, 0:1], in1=wbc.t[0:n, :],
                                                             op0=ALU.mult, op1=ALU.mult),
                     R=[xt, rstd, wbc], W=[act_out])
                return
            P.op("dve", lambda e: e.scalar_tensor_tensor(out=xnb.t[0:n, :], in0=xt.t[0:n, :], scalar=rstd.t[0:n, 0:1],
                                                         in1=wbc.t[0:n, :], op0=ALU.mult, op1=ALU.mult),
                 R=[xt, rstd, wbc], W=[xnb])
            if do_tr:
                transpose_to(i, n, xnb, pT, dstT, dst_bufs)

        def transpose_to(i, n, xnb, pT, dstT, dst_bufs, copy_eng="act"):
            pv = pT.t[:].bitcast(BF16)

            def tr(e):
                last = None
                for k in range(8):
                    last = e.transpose(pv[:, k * 128:k * 128 + n], xnb.t[0:n, k * 128:(k + 1) * 128],
                                       ident_b.t[0:n, 0:n])
                return last
            P.op("pe", tr, R=[xnb, ident_b], W=[pT])
            c0 = i * 128
            if copy_eng == "act":
                P.op("act", lambda e: e.activation(out=dstT[:, :, c0:c0 + n],
                                                   in_=pv.rearrange("p (k t) -> p k t", k=8)[:, :, 0:n],
                                                   func=AF.Copy), R=[pT], W=[dst_bufs[blk_of_tile(i)]])
            else:
                P.op("dve", lambda e: e.tensor_copy(dstT[:, :, c0:c0 + n],
                                                    pv.rearrange("p (k t) -> p k t", k=8)[:, :, 0:n]),
                     R=[pT], W=[dst_bufs[blk_of_tile(i)]])

        with ExitStack() as ph:
            pT = [ps("p1T%d" % j, [128, 512], F32, ph) for j in range(2)]
            c_nmix = sb("c_nmix", [128, 1024], F32, ph)
            P.dma("sp", [(c_nmix.t[:], nmix_bc)], d_cn, W=[c_nmix])
            xt_b = [sb("p1x%d" % j, [128, 1024], F32, ph) for j in range(3)]
            xt_d = [P.dsem() for _ in range(3)]
            xnb_b = [sb("p1n%d" % j, [128, 1024], BF16, ph) for j in range(3)]
            ss_b = [sb("p1s%d" % j, [128, 1], F32, ph) for j in range(2)]
            rs_b = [sb("p1r%d" % j, [128, 1], F32, ph) for j in range(2)]
            def n0(i):
                n = 128 if i < NCH else NS
                src = xp[i * 128:(i + 1) * 128, :] if i < NCH else xs
                xt = xt_b[i % 3]
                P.dma("sp", [(xt.t[0:n, :], src)], xt_d[i % 3], W=[xt])
                rms_scale_transpose(i, xt, n, c_nmix, ss_b[i % 2], rs_b[i % 2], xnb_b[i % 3], None, None, None,
                                    do_tr=False)

            def n1(i):
                n = 128 if i < NCH else NS
                transpose_to(i, n, xnb_b[i % 3], pT[i % 2], xnT, xnT_b, copy_eng="dve")
            run_pipeline(NCH + 1, [n0, n1])
            P.barrier()
        if "xnT" in tapo:
            P.dma("sp", [(tapo["xnT"], xnT[:])], d_out, R=xnT_b)
        if stop_after <= 1:
            P.finish()
            return nc

        NW = 4
        wr_d = [P.dsem() for _ in range(NW)]

        def make_wring(stack, tag, nslots=NW):
            while len(wr_d) < nslots:
                wr_d.append(P.dsem())
            wr_b = [sb("wr%s%d" % (tag, j), [128, 8, 128], BF16, stack) for j in range(nslots)]
            wstate = {"n": 0}

            def load_wtile(src_ap):
                j = wstate["n"] % nslots
                wstate["n"] += 1
                P.dma("pool", [(wr_b[j].t[:], src_ap)], wr_d[j], W=[wr_b[j]])
                return wr_b[j]
            return load_wtile

        def fm_matmul(wt, xT, xT_bufs, banks, bstate, consume, mcols=128):
            for nb in range(5):
                pb = banks[bstate["n"] % len(banks)]
                bstate["n"] += 1
                c0, n = (nb * 512, 512) if nb < 4 else (T, NS)

                def mm(e, pb=pb, c0=c0, n=n):
                    last = None
                    for k in range(8):
                        last = e.matmul(pb.t[0:mcols, 0:n], lhsT=wt.t[:, k, 0:mcols], rhs=xT[:, k, c0:c0 + n],
                                        start=(k == 0), stop=(k == 7))
                    return last
                P.op("pe", mm, R=[wt, xT_bufs[nb]], W=[pb])
                consume(nb, pb, c0, n)

        S3 = es.enter_context(ExitStack())
        x_tok = sb("x_tok", [128, NCH, 1024], BF16, S3)
        BT = sb("BT", [128, 2, T], BF16, S3)
        CT = sb("CT", [128, 2, T], BF16, S3)
        B_tok = sb("B_tok", [128, NCH, 256], BF16, S3)
        dt_tok = sb("dt_tok", [128, NCH, 16], F32, S3)
        da_tok = sb("da_tok", [128, NCH, 16], F32, S3)
        wz_v = S3.enter_context(nc.sbuf_tensor("wz", [128, 8, 1024], BF16))
        wz_b = [Buf(wz_v) for _ in range(4)]

        with ExitStack() as ph:
            banks = [ps("p2b%d" % j, [128, 512], F32, ph) for j in range(6)]
            pTr = ps("p2T", [128, 1024], F32, ph)
            bstate = {"n": 0}
            load_wtile = make_wring(ph, "a")
            stx_b = [sb("p2st%d" % j, [3, 128], F32, ph) for j in range(2)]
            stx_d = [P.dsem() for _ in range(2)]
            wdt = sb("wdt", [128, 8, 16], BF16, ph)
            d_wdt = P.dsem()
            P.dma("pool", [(wdt.t[:], w_dt)], d_wdt, W=[wdt])
            pdt = banks[bstate["n"] % 6]
            bstate["n"] += 1
            pds = banks[bstate["n"] % 6]
            bstate["n"] += 1

            def mmdt(e):
                last = None
                for c in range(NCH):
                    for k in range(8):
                        last = e.matmul(pdt.t[:, c * 16:(c + 1) * 16], lhsT=xnT[:, k, c * 128:(c + 1) * 128],
                                        rhs=wdt.t[:, k, :], start=(k == 0), stop=(k == 7))
                return last
            P.op("pe", mmdt, R=[wdt] + xnT_b[0:4], W=[pdt])

            def mmds(e):
                last = None
                for k in range(8):
                    last = e.matmul(pds.t[0:NS, 0:16], lhsT=xnT[:, k, T:TT], rhs=wdt.t[:, k, :],
                                    start=(k == 0), stop=(k == 7))
                return last
            P.op("pe", mmds, R=[wdt, xnT_b[4]], W=[pds])
            sp_a = sb("sp_a", [128, NCH * 16], F32, ph)
            sp_s = sb("sp_s", [NS, 16], F32, ph)
            dtf = dt_tok.t[:].rearrange("p c h -> p (c h)")
            P.op("dve", lambda e: e.tensor_tensor(out=dt_tok.t[:], in0=pdt.t[:, 0:256].rearrange("p (c h) -> p c h", c=NCH),
                                                  in1=c_dtb.t[:].unsqueeze(1).to_broadcast([128, NCH, 16]), op=ALU.add),
                 R=[pdt, c_dtb], W=[dt_tok])
            P.op("dve", lambda e: e.tensor_tensor(out=dt_s.t[:], in0=pds.t[0:NS, 0:16], in1=c_dtb.t[0:NS, :], op=ALU.add),
                 R=[pds, c_dtb], W=[dt_s])
            for (tt, spb) in ((dt_tok, sp_a), (dt_s, sp_s)):
                tv = dtf if tt is dt_tok else dt_s.t[:]
                P.op("act", lambda e, tv=tv, spb=spb: e.activation(out=spb.t[:], in_=tv, func=AF.Abs), R=[tt], W=[spb])
                P.op("act", lambda e, spb=spb: e.activation(out=spb.t[:], in_=spb.t[:], func=AF.Exp, scale=-1.0),
                     R=[spb], W=[spb])
                P.op("act", lambda e, spb=spb: e.activation(out=spb.t[:], in_=spb.t[:], func=AF.Ln, bias=1.0, scale=1.0),
                     R=[spb], W=[spb])
                P.op("dve", lambda e, tv=tv, spb=spb: e.scalar_tensor_tensor(out=tv, in0=tv, scalar=0.0, in1=spb.t[:],
                                                                             op0=ALU.max, op1=ALU.add),
                     R=[tt, spb], W=[tt])
            P.op("dve", lambda e: e.tensor_tensor(out=da_tok.t[:], in0=dt_tok.t[:],
                                                  in1=c_a.t[:].unsqueeze(1).to_broadcast([128, NCH, 16]),
                                                  op=ALU.mult), R=[dt_tok, c_a], W=[da_tok])

            raw_b = [sb("p2raw%d" % j, [128, 3 + T], F32, ph) for j in range(2)]
            acc_b = [sb("p2acc%d" % j, [128, T], F32, ph) for j in range(2)]
            sil_b = [Buf(acc_b[j].t) for j in range(2)]
            for j in range(2):
                sil_b[j] = acc_b[j]
            for j in range(2):
                P.op("pool", lambda e, j=j: e.memset(raw_b[j].t[:, 0:3], 0.0), W=[raw_b[j]])
            pv = pTr.t[:].bitcast(BF16)

            def a_stage(m):
                wt = load_wtile(w_in_t[m])
                raw = raw_b[m % 2]

                def xbc_consume(nb, pb, c0, n):
                    if nb < 4:
                        P.op("act", lambda e: e.activation(out=raw.t[:, 3 + c0:3 + c0 + n], in_=pb.t[:, 0:n],
                                                           func=AF.Copy), R=[pb], W=[raw])
                    else:
                        P.op("act", lambda e: e.activation(out=raws_x.t[:, m, 3, :], in_=pb.t[:, 0:NS],
                                                           func=AF.Copy), R=[pb], W=[raws_x])
                fm_matmul(wt, xnT, xnT_b, banks, bstate, xbc_consume)

            def b1_stage(m):
                raw, acc = raw_b[m % 2], acc_b[m % 2]
                P.op("act", lambda e: e.activation(
                    out=acc.t[:], in_=raw.t[:, 0:T], func=AF.Identity, bias=c_cbx.t[:, m:m + 1],
                    scale=c_cwx.t[:, m, 0:1]), R=[raw, c_cwx, c_cbx], W=[acc])
                for k in range(1, 4):
                    P.op("dve", lambda e, k=k: e.scalar_tensor_tensor(
                        out=acc.t[:], in0=raw.t[:, k:k + T], scalar=c_cwx.t[:, m, k:k + 1], in1=acc.t[:],
                        op0=ALU.mult, op1=ALU.add), R=[raw, acc, c_cwx], W=[acc])
                pst = banks[bstate["n"] % 6]
                bstate["n"] += 1
                P.op("pe", lambda e: e.transpose(pst.t[0:3, 0:128], raw.t[:, T:T + 3], ident_f.t[:]),
                     R=[raw, ident_f], W=[pst])
                stx = stx_b[m % 2]
                P.op("act", lambda e: e.activation(out=stx.t[:], in_=pst.t[0:3, 0:128], func=AF.Copy),
                     R=[pst], W=[stx])
                P.dma("sp", [(xbc_p[:, m * 128:(m + 1) * 128], stx.t[:])], stx_d[m % 2], R=[stx])

            def dst_of(m):
                if m < 8:
                    return acc_b[m % 2].t[:].bitcast(BF16)[:, 0:T], acc_b[m % 2]
                if m < 10:
                    return BT.t[:, m - 8, :], BT
                return CT.t[:, m - 10, :], CT

            def b2_stage(m):
                acc = acc_b[m % 2]
                dst, dstb = dst_of(m)
                P.op("act", lambda e: e.activation(out=dst, in_=acc.t[:], func=AF.Silu), R=[acc], W=[dstb])

            def c_stage(m):
                if m >= 10:
                    return
                dst, dstb = dst_of(m)

                def trx(e):
                    last = None
                    for c in range(NCH):
                        last = e.transpose(pv[:, c * 128:(c + 1) * 128], dst[:, c * 128:(c + 1) * 128], ident_b.t[:])
                    return last
                P.op("pe", trx, R=[dstb, ident_b], W=[pTr])
                if m < 8:
                    P.op("dve", lambda e: e.tensor_copy(x_tok.t[:, :, m * 128:(m + 1) * 128],
                                                        pv.rearrange("p (c f) -> p c f", c=NCH)),
                         R=[pTr], W=[x_tok])
                else:
                    g = m - 8
                    P.op("dve", lambda e: e.tensor_copy(B_tok.t[:, :, g * 128:(g + 1) * 128],
                                                        pv.rearrange("p (c f) -> p c f", c=NCH)),
                         R=[pTr], W=[B_tok])
            def a_stage_w(m):
                a_stage(m)
                if 4 <= m < 8:
                    q = m - 4
                    P.dma("pool", [(wz_v[:, 2 * q:2 * q + 2, :], w_z[:, 2 * q:2 * q + 2, :])], P.dsem(), W=[wz_b[q]])
            run_pipeline(12, [(a_stage_w, 0), (c_stage, 3), (b1_stage, 1), (b2_stage, 2)])

            sbuf0 = acc_b[0]
            sacc_v = sbuf0.t[:, 0:12 * NS].rearrange("p (m t) -> p m t", m=12)
            stmp_v = sbuf0.t[:, 256:256 + 12 * NS].rearrange("p (m t) -> p m t", m=12)
            for k in range(4):
                wk = c_cwx.t[:, :, k:k + 1].to_broadcast([128, 12, NS])
                if k == 0:
                    P.op("dve", lambda e, wk=wk: e.tensor_tensor(out=sacc_v, in0=raws_x.t[:, :, 0, :], in1=wk,
                                                                 op=ALU.mult), R=[raws_x, c_cwx], W=[sbuf0])
                else:
                    P.op("dve", lambda e, wk=wk, k=k: e.tensor_tensor(out=stmp_v, in0=raws_x.t[:, :, k, :], in1=wk,
                                                                      op=ALU.mult), R=[raws_x, c_cwx, sbuf0], W=[sbuf0])
                    P.op("dve", lambda e: e.tensor_tensor(out=sacc_v, in0=sacc_v, in1=stmp_v, op=ALU.add),
                         R=[sbuf0], W=[sbuf0])
            P.op("dve", lambda e: e.tensor_tensor(out=sacc_v, in0=sacc_v,
                                                  in1=c_cbx.t[:].unsqueeze(2).to_broadcast([128, 12, NS]),
                                                  op=ALU.add), R=[sbuf0, c_cbx], W=[sbuf0])
            P.op("act", lambda e: e.activation(out=xs_T.t[:], in_=sacc_v, func=AF.Silu), R=[sbuf0], W=[xs_T])
            sxs_b = [sb("p2sx%d" % j, [NS, 512], F32, ph) for j in range(1)]
            sxs_d = [P.dsem() for _ in range(1)]
            for q in range(3):
                pst = banks[bstate["n"] % 6]
                bstate["n"] += 1
                sxs = sxs_b[0]

                def trq(e, pst=pst, q=q):
                    last = None
                    for jj in range(4):
                        last = e.transpose(pst.t[0:NS, jj * 128:(jj + 1) * 128], raws_x.t[:, q * 4 + jj, 3, :],
                                           ident_f.t[:])
                    return last
                P.op("pe", trq, R=[raws_x, ident_f], W=[pst])
                P.op("act", lambda e, pst=pst, sxs=sxs: e.activation(out=sxs.t[:], in_=pst.t[0:NS, :], func=AF.Copy),
                     R=[pst], W=[sxs])
                P.dma("sp", [(xbc_s[:, 2, q * 512:(q + 1) * 512], sxs.t[:])], sxs_d[0], R=[sxs])
            P.barrier()
        for nm, b in (("x_tok", x_tok), ("BT", BT), ("CT", CT), ("B_tok", B_tok), ("dt_tok", dt_tok),
                      ("xs_T", xs_T)):
            if nm in tapo:
                P.dma("sp", [(tapo[nm], b.t[:])], d_out, R=[b])
        if stop_after <= 2:
            P.finish()
            return nc


        with ExitStack() as ph:
            SEGa = ps("SEGa", [128, 1024], F32, ph)
            SEGb = ps("SEGb", [128, 1024], F32, ph)
            Y1 = ps("Y1", [128, 1024], F32, ph)
            Y2a = ps("Y2a", [128, 512], F32, ph)
            Y2b = ps("Y2b", [128, 512], F32, ph)
            P.dma("pool", [(wo_bf[:, 8 * q:8 * q + 8, :], w_out_t[:, 8 * q:8 * q + 8, :]) for q in range(2)], d_pc,
                  W=[wo_bf_b])
            P.dma("pool", [(wd_bf[:, 11 * q:11 * q + 11, :], w_down_t[:, 11 * q:11 * q + 11, :]) for q in range(2)],
                  P.dsem(), W=[wd_bf_b])
            hT = sb("hT", [128, 1024], F32, ph)
            hTb = sb("hTb", [128, 1024], BF16, ph)
            Rhi = sb("Rhi", [128, 2048], BF16, ph)
            lst_b = sb("lst_b", [128, 128], BF16, ph)
            P.op("dve", lambda e: e.tensor_copy(lst_b.t[:], lst_f.t[:]), R=[lst_f], W=[lst_b])
            tD_b = [sb("tD%d" % j, [128, 1024], BF16, ph) for j in range(2)]
            dec_all = sb("dec_all", [128, 3, NCH * 16], F32, ph)
            da_all = da_tok.t[:].rearrange("p c h -> p (c h)")

            def s1all(e):
                e.matmul(SEGa.t[:, 0:256], lhsT=lst_f.t[:], rhs=da_all, start=True, stop=True)
                e.matmul(SEGa.t[:, 256:512], lhsT=ones_f.t[:], rhs=da_all, start=True, stop=True)
                return e.matmul(SEGa.t[:, 512:768], lhsT=tri_f.t[:], rhs=da_all, start=True, stop=True)
            P.op("pe", s1all, R=[da_tok, lst_f, ones_f, tri_f], W=[SEGa])
            P.op("act", lambda e: e.activation(out=dec_all.t[:].rearrange("p a b -> p (a b)"), in_=SEGa.t[:, 0:768],
                                               func=AF.Exp), R=[SEGa], W=[dec_all])
            CBm = sb("CBm", [128, 256], BF16, ph)
            Mb_b = [sb("Mb%d" % j, [128, 2048], BF16, ph) for j in range(2)]
            xd_b = [sb("xd%d" % j, [128, 1024], BF16, ph) for j in range(2)]
            xdd_b = [sb("xdd%d" % j, [128, 1024], BF16, ph) for j in range(2)]
            t1 = sb("t1", [128, 1024], F32, ph)
            th = sb("th", [128, 1024], F32, ph)
            yn = sb("yn", [128, 1024], BF16, ph)
            ss2 = sb("ss2", [128, 2], F32, ph)
            rs2 = sb("rs2", [128, 2], F32, ph)
            stage_ssm = t1
            print("SSD phase sbuf remaining", nc.sbuf_bytes_remaining)

            def bc3(ap2, n_mid, n_in):
                return ap2.unsqueeze(2).to_broadcast([128, n_mid, n_in])

            tri_b = sb("tri_b", [128, 128], BF16, ph)
            P.op("dve", lambda e: e.tensor_copy(tri_b.t[:], tri_f.t[:]), R=[tri_f], W=[tri_b])

            def front(c):
                cs = slice(c * 128, (c + 1) * 128)
                Mb, xd, xdd = Mb_b[c % 2], xd_b[c % 2], xdd_b[c % 2]
                dcs = slice(c * 16, (c + 1) * 16)

                def s6(e):
                    last = None
                    for g in range(2):
                        last = e.matmul(Y2b.t[:, g * 128:(g + 1) * 128], lhsT=BT.t[:, g, cs], rhs=CT.t[:, g, cs],
                                        start=True, stop=True)
                    return last
                P.op("pe", s6, R=[BT, CT], W=[Y2b])

                def s3(e):
                    last = None
                    for h in range(16):
                        last = e.tensor_scalar(out=Rhi.t[:, h * 128:(h + 1) * 128], in0=tri_b.t[:],
                                               scalar1=da_tok.t[:, c, h:h + 1], scalar2=None, op0=ALU.mult)
                    return last
                P.op("dve", s3, R=[da_tok, tri_b], W=[Rhi])
                for hf, SEG in ((0, SEGa), (1, SEGb)):
                    def s4(e, hf=hf, SEG=SEG):
                        last = None
                        for q in range(2):
                            cols = slice(hf * 1024 + q * 512, hf * 1024 + (q + 1) * 512)
                            last = e.matmul(SEG.t[:, q * 512:(q + 1) * 512], lhsT=lst_b.t[:], rhs=Rhi.t[:, cols],
                                            start=True, stop=True)
                        return last
                    P.op("pe", s4, R=[Rhi, lst_b], W=[SEG])
                    P.op("act", lambda e, hf=hf, SEG=SEG: e.activation(out=Mb.t[:, hf * 1024:(hf + 1) * 1024],
                                                                       in_=SEG.t[:], func=AF.Exp),
                         R=[SEG], W=[Mb])
                P.op("dve", lambda e: e.tensor_tensor(
                    out=CBm.t[:].rearrange("p (g l) -> p g l", g=2),
                    in0=Y2b.t[:, 0:256].rearrange("p (g l) -> p g l", g=2),
                    in1=tri_f.t[:].unsqueeze(1).to_broadcast([128, 2, 128]), op=ALU.mult),
                    R=[Y2b, tri_f], W=[CBm])
                P.op("dve", lambda e: e.tensor_tensor(
                    out=xd.t[:].rearrange("p (h q) -> p h q", h=16),
                    in0=x_tok.t[:, c, :].rearrange("p (h q) -> p h q", h=16),
                    in1=bc3(dt_tok.t[:, c, :], 16, 64), op=ALU.mult), R=[x_tok, dt_tok], W=[xd])
                P.op("pool", lambda e: e.tensor_tensor(
                    out=xdd.t[:].rearrange("p (h q) -> p h q", h=16),
                    in0=xd.t[:].rearrange("p (h q) -> p h q", h=16),
                    in1=bc3(dec_all.t[:, 0, dcs], 16, 64), op=ALU.mult), R=[xd, dec_all], W=[xdd])

            def pre(c):
                tDc = tD_b[c % 2]
                P.op("pool", lambda e: e.tensor_tensor(out=tDc.t[:], in0=x_tok.t[:, c, :], in1=c_d.t[:],
                                                       op=ALU.mult), R=[x_tok, c_d], W=[tDc])

            def front_b(c):
                Mb = Mb_b[c % 2]
                P.op("dve", lambda e: e.tensor_tensor(
                    out=Mb.t[:].rearrange("p (g h l) -> p g h l", g=2, h=8),
                    in0=Mb.t[:].rearrange("p (g h l) -> p g h l", g=2, h=8),
                    in1=CBm.t[:].rearrange("p (g l) -> p g l", g=2).unsqueeze(2).to_broadcast([128, 2, 8, 128]),
                    op=ALU.mult), R=[Mb, CBm], W=[Mb])

            def rec(c):
                cs = slice(c * 128, (c + 1) * 128)
                xdd = xdd_b[c % 2]
                if c > 0:
                    for g, Y2 in ((0, Y2a), (1, Y2b)):
                        P.op("pe", lambda e, g=g, Y2=Y2: e.matmul(
                            Y2.t[:, 0:512], lhsT=CT.t[:, g, cs], rhs=hTb.t[:, g * 512:(g + 1) * 512],
                            start=True, stop=True), R=[CT, hTb], W=[Y2])
                    P.op("pool", lambda e: e.tensor_tensor(
                        out=hT.t[:].rearrange("p (h q) -> p h q", h=16),
                        in0=hT.t[:].rearrange("p (h q) -> p h q", h=16),
                        in1=bc3(dec_all.t[:, 1, c * 16:(c + 1) * 16], 16, 64), op=ALU.mult), R=[hT, dec_all], W=[hT])

                def s14(e):
                    last = None
                    for g in range(2):
                        last = e.matmul(SEGb.t[:, g * 512:(g + 1) * 512], lhsT=B_tok.t[:, c, g * 128:(g + 1) * 128],
                                        rhs=xdd.t[:, g * 512:(g + 1) * 512], start=True, stop=True)
                    return last
                P.op("pe", s14, R=[B_tok, xdd], W=[SEGb])
                if c > 0:
                    P.op("dve", lambda e: e.tensor_tensor(out=hT.t[:], in0=SEGb.t[:], in1=hT.t[:], op=ALU.add),
                         R=[SEGb, hT], W=[hT])
                else:
                    P.op("dve", lambda e: e.tensor_copy(hT.t[:], SEGb.t[:]), R=[SEGb], W=[hT])
                if c < NCH - 1:
                    P.op("act", lambda e: e.activation(out=hTb.t[:], in_=hT.t[:], func=AF.Copy), R=[hT], W=[hTb])

            def tail_a(c):
                cs = slice(c * 128, (c + 1) * 128)
                Mb, xd = Mb_b[c % 2], xd_b[c % 2]
                tD = tD_b[c % 2]
                if c > 0:
                    for g, Y2 in ((0, Y2a), (1, Y2b)):
                        P.op("dve", lambda e, g=g, Y2=Y2: e.tensor_tensor(
                            out=t1.t[:, g * 512:(g + 1) * 512].rearrange("p (h q) -> p h q", h=8),
                            in0=Y2.t[:, 0:512].rearrange("p (h q) -> p h q", h=8),
                            in1=bc3(dec_all.t[:, 2, c * 16 + 8 * g:c * 16 + 8 * g + 8], 8, 64), op=ALU.mult),
                            R=[Y2, dec_all], W=[t1])

                def s13(e):
                    last = None
                    for hf in range(2):
                        for k in range(8):
                            last = e.matmul(SEGa.t[:, hf * 512:(hf + 1) * 512], lhsT=xnT[:, k, cs],
                                            rhs=wz_v[:, k, hf * 512:(hf + 1) * 512], start=(k == 0), stop=(k == 7))
                    return last
                P.op("pe", s13, R=[xnT_b[c // 4]] + wz_b, W=[SEGa])
                P.op("act", lambda e: e.activation(out=th.t[:], in_=SEGa.t[:], func=AF.Silu), R=[SEGa], W=[th])

                def s11(e):
                    last = None
                    for hf in range(2):
                        e.matmul(Y1.t[:, hf * 512:(hf + 1) * 512], lhsT=ident_b.t[:], rhs=tD.t[:, hf * 512:(hf + 1) * 512],
                                 start=True, stop=False, skip_group_check=True)
                    for h in range(16):
                        last = e.matmul(Y1.t[:, h * 64:(h + 1) * 64], lhsT=Mb.t[:, h * 128:(h + 1) * 128],
                                        rhs=xd.t[:, h * 64:(h + 1) * 64], start=False, stop=(h % 8 == 7),
                                        skip_group_check=True)
                    return last
                P.op("pe", s11, R=[Mb, xd, tD, ident_b], W=[Y1])
                if c > 0:
                    P.op("dve", lambda e: e.tensor_tensor(out=t1.t[:], in0=Y1.t[:], in1=t1.t[:], op=ALU.add),
                         R=[Y1, t1], W=[t1])
                else:
                    P.op("dve", lambda e: e.tensor_copy(t1.t[:], Y1.t[:]), R=[Y1], W=[t1])
                P.op("dve", lambda e: e.tensor_tensor(out=t1.t[:], in0=t1.t[:], in1=th.t[:], op=ALU.mult),
                     R=[t1, th], W=[t1])
                for g in range(2):
                    P.op("act", lambda e, g=g: e.activation(out=th.t[:, g * 512:(g + 1) * 512],
                                                            in_=t1.t[:, g * 512:(g + 1) * 512], func=AF.Square,
                                                            accum_out=ss2.t[:, g:g + 1]), R=[t1], W=[th, ss2])
                P.op("pool", lambda e: e.tensor_scalar(out=rs2.t[:], in0=ss2.t[:], scalar1=1.0 / 512, scalar2=EPS,
                                                       op0=ALU.mult, op1=ALU.add), R=[ss2], W=[rs2])
                P.op("pool", lambda e: e.tensor_tensor(out=rs2.t[:], in0=rs2.t[:], in1=mhalf.t[:, 0:2], op=ALU.pow),
                     R=[rs2, mhalf], W=[rs2])

            def tail_b(c):
                for g in range(2):
                    P.op("dve", lambda e, g=g: e.scalar_tensor_tensor(
                        out=yn.t[:, g * 512:(g + 1) * 512], in0=t1.t[:, g * 512:(g + 1) * 512],
                        scalar=rs2.t[:, g:g + 1], in1=c_nssd.t[:, g * 512:(g + 1) * 512], op0=ALU.mult, op1=ALU.mult),
                        R=[t1, rs2, c_nssd], W=[yn])

            def post(c):
                cs = slice(c * 128, (c + 1) * 128)
                pv = Y2b.t[:].bitcast(BF16)

                def s24(e):
                    last = None
                    for j in range(8):
                        last = e.transpose(pv[:, j * 128:(j + 1) * 128], yn.t[:, j * 128:(j + 1) * 128], ident_b.t[:])
                    return last
                P.op("pe", s24, R=[yn, ident_b], W=[Y2b])
                P.op("act", lambda e: e.activation(out=ycat_ssd.t[:, :, cs],
                                                   in_=pv.rearrange("p (j t) -> p j t", j=8), func=AF.Copy),
                     R=[Y2b], W=[ycat_ssd])
            run_pipeline(NCH, [(rec, 2), (pre, 1), (tail_a, 2), (front, 1), (post, 3), (tail_b, 2), (front_b, 1)])

            def sfin(e):
                last = None
                for j in range(8):
                    last = e.transpose(SEGa.t[:, j * 128:(j + 1) * 128], hT.t[:, j * 128:(j + 1) * 128], ident_f.t[:])
                return last
            P.op("pe", sfin, R=[hT, ident_f], W=[SEGa])
            P.op("act", lambda e: e.activation(out=stage_ssm.t[:], in_=SEGa.t[:], func=AF.Copy),
                 R=[SEGa], W=[stage_ssm])
            P.dma("sp", [(ssm_p.rearrange("(j q) n -> q j n", q=128),
                          stage_ssm.t[:].rearrange("p (j n) -> p j n", j=8))], d_out, R=[stage_ssm])

            def szs(e):
                last = None
                for hf in range(2):
                    for k in range(8):
                        last = e.matmul(SEGb.t[0:NS, hf * 512:(hf + 1) * 512], lhsT=xnT[:, k, T:TT],
                                        rhs=wz_v[:, k, hf * 512:(hf + 1) * 512], start=(k == 0), stop=(k == 7))
                return last
            P.op("pe", szs, R=[xnT_b[4]] + wz_b, W=[SEGb])
            P.op("act", lambda e: e.activation(out=us_tok.t[:], in_=SEGb.t[0:NS, :], func=AF.Silu),
                 R=[SEGb], W=[us_tok])
            P.barrier()
        if "ycat_ssd" in tapo:
            P.dma("sp", [(tapo["ycat_ssd"], ycat_ssd.t[:])], d_out, R=[ycat_ssd])
        if stop_after <= 3:
            P.finish()
            return nc
        S3.close()
        S1b = es.enter_context(ExitStack())
        ycat_sc = sb("ycat_sc", [128, 8, TT], BF16, S1b)
        arena = S1b.enter_context(nc.sbuf_tensor("arena2b", [128, 10248], F32))
        wo_v = arena[:, 0:8192].bitcast(BF16).rearrange("p (k c) -> p k c", k=16)
        wo_b = [Buf(wo_v) for _ in range(4)]

        def bcm(ap2, n_mid, n_in, npart=128):
            return ap2.unsqueeze(2).to_broadcast([npart, n_mid, n_in])

        scr_dec = Buf(nc.dram_tensor("scr_dec", [NS, 16], F32, kind="Internal").ap())
        scr_dtx = Buf(nc.dram_tensor("scr_dtx", [NS, 1024], F32, kind="Internal").ap())
        scr_B = Buf(nc.dram_tensor("scr_B", [NS, 1024], F32, kind="Internal").ap())
        scr_C = Buf(nc.dram_tensor("scr_C", [NS, 1024], F32, kind="Internal").ap())
        scr_y = Buf(ys_d)
        with ExitStack() as ph:
            banks = [ps("p3b%d" % j, [128, 512], F32, ph) for j in range(7)]
            pY = ps("p3y", [128, 512], F32, ph)
            bstate = {"n": 0}

            def nbank():
                b = banks[bstate["n"] % len(banks)]
                bstate["n"] += 1
                return b
            load_wtile = make_wring(ph, "b")
            sgc = sb("sgc", [128, 8, NS], F32, ph)
            sgb = sb("sgb", [128, 8, NS], F32, ph)
            stc_b = [sb("p3st%d" % j, [2, 128], F32, ph) for j in range(2)]
            stc_d = [P.dsem() for _ in range(2)]

            xs_tok = sb("xs_tok", [NS, 1024], F32, ph)
            BC_s = sb("BC_s", [NS, 512], F32, ph)
            dec_s = sb("dec_s", [NS, 16], F32, ph)
            dtx_s = sb("dtx_s", [NS, 1024], F32, ph)
            for half in range(2):
                pb = nbank()

                def trs(e, pb=pb, half=half):
                    last = None
                    for jj in range(4):
                        last = e.transpose(pb.t[0:NS, jj * 128:(jj + 1) * 128], xs_T.t[:, half * 4 + jj, :], ident_f.t[:])
                    return last
                P.op("pe", trs, R=[xs_T, ident_f], W=[pb])
                P.op("act", lambda e, pb=pb, half=half: e.activation(out=xs_tok.t[:, half * 512:(half + 1) * 512],
                                                                     in_=pb.t[0:NS, :], func=AF.Copy),
                     R=[pb], W=[xs_tok])
            pb = nbank()

            def trbc(e, pb=pb):
                last = None
                for jj in range(4):
                    last = e.transpose(pb.t[0:NS, jj * 128:(jj + 1) * 128], xs_T.t[:, 8 + jj, :], ident_f.t[:])
                return last
            P.op("pe", trbc, R=[xs_T, ident_f], W=[pb])
            P.op("act", lambda e, pb=pb: e.activation(out=BC_s.t[:], in_=pb.t[0:NS, :], func=AF.Copy), R=[pb], W=[BC_s])
            P.op("dve", lambda e: e.tensor_tensor(out=dec_s.t[:], in0=dt_s.t[:], in1=c_a.t[0:NS, :], op=ALU.mult),
                 R=[dt_s, c_a], W=[dec_s])
            P.op("act", lambda e: e.activation(out=dec_s.t[:], in_=dec_s.t[:], func=AF.Exp), R=[dec_s], W=[dec_s])
            P.op("dve", lambda e: e.tensor_tensor(
                out=dtx_s.t[:].rearrange("p (h q) -> p h q", h=16), in0=xs_tok.t[:].rearrange("p (h q) -> p h q", h=16),
                in1=bcm(dt_s.t[:], 16, 64, NS), op=ALU.mult), R=[xs_tok, dt_s], W=[dtx_s])
            d_scr = P.dsem()
            P.dma("sp", [(scr_dec.t, dec_s.t[:]), (scr_dtx.t, dtx_s.t[:])], d_scr, R=[dec_s, dtx_s],
                  W=[scr_dec, scr_dtx])
            prs = []
            for g in range(2):
                prs.append((scr_B.t[:, g * 512:(g + 1) * 512].rearrange("b (r n) -> b r n", r=4),
                            BC_s.t[:, g * 128:(g + 1) * 128].unsqueeze(1).to_broadcast([NS, 4, 128])))
                prs.append((scr_C.t[:, g * 512:(g + 1) * 512].rearrange("b (r n) -> b r n", r=4),
                            BC_s.t[:, 256 + g * 128:256 + (g + 1) * 128].unsqueeze(1).to_broadcast([NS, 4, 128])))
            P.dma("sp", prs, d_scr, R=[BC_s], W=[scr_B, scr_C])
            dec_q = sb("dec_q", [128, 2], F32, ph)
            dtx_q = sb("dtx_q", [128, 128], F32, ph)
            B_q = sb("B_q", [128, 128], F32, ph)
            C_q = sb("C_q", [128, 128], F32, ph)
            y_q = sb("y_q", [128, 128], F32, ph)
            d_q = P.dsem()
            P.dma("sp", [(dec_q.t[:], scr_dec.t.rearrange("b (q r) -> (b q) r", q=8)),
                         (dtx_q.t[:], scr_dtx.t.rearrange("b (q r) -> (b q) r", q=8)),
                         (B_q.t[:], scr_B.t.rearrange("b (q r) -> (b q) r", q=8)),
                         (C_q.t[:], scr_C.t.rearrange("b (q r) -> (b q) r", q=8))], d_q,
                  R=[scr_dec, scr_dtx, scr_B, scr_C], W=[dec_q, dtx_q, B_q, C_q])
            hq_b = [sb("hq%d" % j, [128, 1024], F32, ph) for j in range(2)]
            ob_b = [sb("ob%d" % j, [128, 1024], F32, ph) for j in range(2)]
            hq_d = [P.dsem() for _ in range(2)]
            ob_d = [P.dsem() for _ in range(2)]
            phl = ph.enter_context(ExitStack())
            acc_b = [Buf(arena[:, 0:2048]), Buf(arena[:, 2048:4096])]
            gc_b = [Buf(arena[:, 4096:6144])]
            u_b = [Buf(arena[:, 6144:8194]), Buf(arena[:, 8194:10244])]
            for j in range(2):
                P.op("pool", lambda e, j=j: e.memset(u_b[j].t[:, 0:2], 0.0), W=[u_b[j]])

            def sample_state_piece(i):
                hq, ob = hq_b[i % 2], ob_b[i % 2]
                cols = slice(i * 1024, (i + 1) * 1024)
                hh = i // 8
                r0 = i * 8
                P.dma("sp", [(hq.t[:], st_ssm[:, cols])], hq_d[i % 2], W=[hq])
                def outer(e):
                    last = None
                    for r in range(8):
                        last = e.activation(out=ob.t[:, r * 128:(r + 1) * 128], in_=B_q.t[:], func=AF.Copy,
                                            scale=dtx_q.t[:, r0 + r:r0 + r + 1])
                    return last
                P.op("act", outer, R=[dtx_q, B_q], W=[ob])
                P.op("dve", lambda e: e.scalar_tensor_tensor(out=ob.t[:], in0=hq.t[:], scalar=dec_q.t[:, hh:hh + 1],
                                                             in1=ob.t[:], op0=ALU.mult, op1=ALU.add),
                     R=[hq, dec_q, ob], W=[ob])
                P.dma("sp", [(ssm_s[:, cols], ob.t[:])], ob_d[i % 2], R=[ob])
                P.op("dve", lambda e: e.tensor_tensor(
                    out=hq.t[:].rearrange("p (r n) -> p r n", r=8), in0=ob.t[:].rearrange("p (r n) -> p r n", r=8),
                    in1=C_q.t[:].unsqueeze(1).to_broadcast([128, 8, 128]), op=ALU.mult),
                    R=[ob, C_q], W=[hq])
                P.op("dve", lambda e: e.tensor_reduce(out=y_q.t[:, r0:r0 + 8],
                                                      in_=hq.t[:].rearrange("p (r n) -> p r n", r=8),
                                                      axis=AX.X, op=ALU.add), R=[hq], W=[y_q])

            def sample_y_tail():
                P.dma("sp", [(scr_y.t, y_q.t[:])], d_scr, R=[y_q], W=[scr_y])
                ys_tok = sb("ys_tok", [NS, 1024], F32, ph)
                ystmp = Alias(hq_b[0], hq_b[0].t[0:NS, 0:1024])
                yns = Alias(ob_b[0], ob_b[0].t[0:NS, 0:512].bitcast(BF16))
                ss3 = sb("ss3", [NS, 2], F32, ph)
                rs3 = sb("rs3", [NS, 2], F32, ph)
                P.dma("sp", [(ys_tok.t[:], scr_y.t.rearrange("(b q) r -> b (q r)", q=8))], d_q, R=[scr_y], W=[ys_tok])
                P.op("dve", lambda e: e.tensor_tensor(out=ystmp.t[:], in0=xs_tok.t[:], in1=c_d.t[0:NS, :], op=ALU.mult),
                     R=[xs_tok, c_d], W=[ystmp])
                P.op("dve", lambda e: e.tensor_tensor(out=ys_tok.t[:], in0=ys_tok.t[:], in1=ystmp.t[:], op=ALU.add),
                     R=[ys_tok, ystmp], W=[ys_tok])
                P.op("dve", lambda e: e.tensor_tensor(out=ys_tok.t[:], in0=ys_tok.t[:], in1=us_tok.t[:], op=ALU.mult),
                     R=[ys_tok, us_tok], W=[ys_tok])
                for g in range(2):
                    P.op("act", lambda e, g=g: e.activation(out=yns.t[:, g * 512:(g + 1) * 512],
                                                            in_=ys_tok.t[:, g * 512:(g + 1) * 512], func=AF.Square,
                                                            accum_out=ss3.t[:, g:g + 1]), R=[ys_tok], W=[yns, ss3])
                P.op("pool", lambda e: e.tensor_scalar(out=rs3.t[:], in0=ss3.t[:], scalar1=1.0 / 512, scalar2=EPS,
                                                       op0=ALU.mult, op1=ALU.add), R=[ss3], W=[rs3])
                P.op("pool", lambda e: e.tensor_tensor(out=rs3.t[:], in0=rs3.t[:], in1=mhalf.t[0:NS, 0:2], op=ALU.pow),
                     R=[rs3, mhalf], W=[rs3])
                for g in range(2):
                    P.op("dve", lambda e, g=g: e.scalar_tensor_tensor(
                        out=yns.t[:, g * 512:(g + 1) * 512], in0=ys_tok.t[:, g * 512:(g + 1) * 512],
                        scalar=rs3.t[:, g:g + 1], in1=c_nssd.t[0:NS, g * 512:(g + 1) * 512], op0=ALU.mult, op1=ALU.mult),
                        R=[ys_tok, rs3, c_nssd], W=[yns])
                pvs = pY.t[:].bitcast(BF16)

                def trys(e):
                    last = None
                    for j in range(8):
                        last = e.transpose(pvs[:, j * NS:(j + 1) * NS], yns.t[0:NS, j * 128:(j + 1) * 128],
                                           ident_b.t[0:NS, 0:NS])
                    return last
                P.op("pe", trys, R=[yns, ident_b], W=[pY])
                P.op("act", lambda e: e.activation(out=ycat_ssd.t[:, :, T:TT],
                                                   in_=pvs[:, 0:8 * NS].rearrange("p (j t) -> p j t", j=8), func=AF.Copy),
                     R=[pY], W=[ycat_ssd])

            def sc0(m):
                gc, u = gc_b[0], u_b[m % 2]
                wt = load_wtile(w_in_t[12 + 3 * m])

                def gc_consume(nb, pb, c0, n):
                    if nb < 4:
                        P.op("act", lambda e: e.activation(out=gc.t[:, c0:c0 + n], in_=pb.t[:, 0:n], func=AF.Copy),
                             R=[pb], W=[gc])
                    else:
                        P.op("act", lambda e: e.activation(out=sgc.t[:, m, :], in_=pb.t[:, 0:NS], func=AF.Copy),
                             R=[pb], W=[sgc])
                fm_matmul(wt, xnT, xnT_b, banks, bstate, gc_consume)
                wt = load_wtile(w_in_t[12 + 3 * m + 1])

                def h_consume(nb, pb, c0, n):
                    if nb < 4:
                        P.op("dve", lambda e: e.tensor_tensor(out=u.t[:, 2 + c0:2 + c0 + n], in0=pb.t[:, 0:n],
                                                              in1=gc.t[:, c0:c0 + n], op=ALU.mult),
                             R=[pb, gc], W=[u])
                    else:
                        P.op("dve", lambda e: e.tensor_tensor(out=raws_s.t[:, m, 2, :], in0=pb.t[:, 0:NS],
                                                              in1=sgc.t[:, m, :], op=ALU.mult),
                             R=[pb, sgc], W=[raws_s])
                fm_matmul(wt, xnT, xnT_b, banks, bstate, h_consume)

            def sc1(m):
                u, acc = u_b[m % 2], acc_b[m % 2]
                P.op("act", lambda e: e.activation(out=acc.t[:], in_=u.t[:, 0:T], func=AF.Identity,
                                                   scale=c_cws.t[:, m, 0:1]), R=[u, c_cws], W=[acc])
                for k in range(1, 3):
                    P.op("dve", lambda e, k=k: e.scalar_tensor_tensor(
                        out=acc.t[:], in0=u.t[:, k:k + T], scalar=c_cws.t[:, m, k:k + 1], in1=acc.t[:],
                        op0=ALU.mult, op1=ALU.add), R=[u, acc, c_cws], W=[acc])
                pst = nbank()
                P.op("pe", lambda e: e.transpose(pst.t[0:2, 0:128], u.t[:, T:T + 2], ident_f.t[:]),
                     R=[u, ident_f], W=[pst])
                stc = stc_b[m % 2]
                P.op("act", lambda e: e.activation(out=stc.t[:], in_=pst.t[0:2, 0:128], func=AF.Copy),
                     R=[pst], W=[stc])
                P.dma("sp", [(sc_p[:, m * 128:(m + 1) * 128], stc.t[:])], stc_d[m % 2], R=[stc])

            def sc2(m):
                acc = acc_b[m % 2]
                wt = load_wtile(w_in_t[12 + 3 * m + 2])

                def gb_consume(nb, pb, c0, n):
                    if nb < 4:
                        P.op("dve", lambda e: e.tensor_tensor(out=ycat_sc.t[:, m, c0:c0 + n], in0=pb.t[:, 0:n],
                                                              in1=acc.t[:, c0:c0 + n], op=ALU.mult),
                             R=[pb, acc], W=[ycat_sc])
                    else:
                        P.op("act", lambda e: e.activation(out=sgb.t[:, m, :], in_=pb.t[:, 0:NS], func=AF.Copy),
                             R=[pb], W=[sgb])
                fm_matmul(wt, xnT, xnT_b, banks, bstate, gb_consume)
                sched = {0: [0, 1, 2], 1: [3, 4, 5], 2: [6, 7], 3: [8, 9], 4: [10, 11], 5: [12, 13], 6: [14, 15]}
                for i in sched.get(m, []):
                    sample_state_piece(i)
                if m == 6:
                    sample_y_tail()
            run_pipeline(8, [sc0, sc1, sc2])
            for q in range(4):
                P.dma("act", [(wo_v[:, 4 * q:4 * q + 4, :], wo_bf[:, 4 * q:4 * q + 4, :])], P.dsem(), R=[wo_bf_b],
                      W=[wo_b[q]] + ([acc_b[0], acc_b[1], gc_b[0], u_b[0]] if q == 0 else []))
            stage_scs = Alias(u_b[1], u_b[1].t[0:NS, 0:1024])

            sacc = sb("sacc2", [128, 8, NS], F32, ph)
            stmp = sb("stmp2", [128, 8, NS], F32, ph)
            for k in range(3):
                wk = c_cws.t[:, :, k:k + 1].to_broadcast([128, 8, NS])
                if k == 0:
                    P.op("dve", lambda e, wk=wk: e.tensor_tensor(out=sacc.t[:], in0=raws_s.t[:, :, 0, :], in1=wk,
                                                                 op=ALU.mult), R=[raws_s, c_cws], W=[sacc])
                else:
                    P.op("dve", lambda e, wk=wk, k=k: e.tensor_tensor(out=stmp.t[:], in0=raws_s.t[:, :, k, :], in1=wk,
                                                                      op=ALU.mult), R=[raws_s, c_cws], W=[stmp])
                    P.op("dve", lambda e: e.tensor_tensor(out=sacc.t[:], in0=sacc.t[:], in1=stmp.t[:], op=ALU.add),
                         R=[sacc, stmp], W=[sacc])
            P.op("dve", lambda e: e.tensor_tensor(out=ycat_sc.t[:, :, T:TT], in0=sacc.t[:], in1=sgb.t[:], op=ALU.mult),
                 R=[sacc, sgb], W=[ycat_sc])
            for half in range(2):
                pb = nbank()

                def tru(e, pb=pb, half=half):
                    last = None
                    for jj in range(4):
                        last = e.transpose(pb.t[0:NS, jj * 128:(jj + 1) * 128], raws_s.t[:, half * 4 + jj, 2, :],
                                           ident_f.t[:])
                    return last
                P.op("pe", tru, R=[raws_s, ident_f], W=[pb])
                P.op("act", lambda e, pb=pb, half=half: e.activation(out=stage_scs.t[:, half * 512:(half + 1) * 512],
                                                                     in_=pb.t[0:NS, :], func=AF.Copy),
                     R=[pb], W=[stage_scs])
            P.dma("sp", [(sc_s[:, 1, :], stage_scs.t[:])], d_out, R=[stage_scs])

            P.barrier()
        S2.close()
        for nm, b in (("ycat_sc", ycat_sc), ("ycat_ssd2", ycat_ssd)):
            if nm in tapo:
                P.dma("sp", [(tapo[nm], b.t[:])], d_out, R=[b])
        if stop_after <= 4:
            P.finish()
            return nc

        SR = es.enter_context(ExitStack())
        xn2T = SR.enter_context(nc.sbuf_tensor("xn2T", [128, 8, TT], BF16, side="right"))
        xn2T_b = [Buf(xn2T) for _ in range(5)]
        x1_rows = [Buf(x1_d[i * 128:(i + 1) * 128, :]) for i in range(NCH)] + [Buf(x1_d[T:TT, :])]
        with ExitStack() as ph:
            pb4 = [ps("p4b%d" % j, [128, 512], F32, ph) for j in range(4)]
            pT = [ps("p4T%d" % j, [128, 512], F32, ph) for j in range(2)]
            c_nffn = sb("c_nffn", [128, 1024], F32, ph)
            P.dma("sp", [(c_nffn.t[:], nffn_bc)], d_cn, W=[c_nffn])
            xt_b = [sb("p4x%d" % j, [128, 1024], F32, ph) for j in range(3)]
            xt_d = [P.dsem() for _ in range(3)]
            x1_b = [sb("p4y%d" % j, [128, 1024], F32, ph) for j in range(3)]
            x1_dd = [P.dsem() for _ in range(3)]
            xnb_b = [sb("p4n%d" % j, [128, 1024], BF16, ph) for j in range(2)]
            ss_b = [sb("p4s%d" % j, [128, 1], F32, ph) for j in range(2)]
            rs_b = [sb("p4r%d" % j, [128, 1], F32, ph) for j in range(2)]
            def o0(i):
                n = 128 if i < NCH else NS
                c0 = i * 128
                src = xp[c0:c0 + 128, :] if i < NCH else xs
                xt, x1t = xt_b[i % 3], x1_b[i % 3]
                P.dma("sp", [(xt.t[0:n, :], src)], xt_d[i % 3], W=[xt])
                for hf in range(2):
                    pb = pb4[(2 * i + hf) % 4]

                    def mo(e, pb=pb, hf=hf):
                        last = None
                        for k in range(16):
                            lhsT = ycat_ssd.t[:, k, c0:c0 + n] if k < 8 else ycat_sc.t[:, k - 8, c0:c0 + n]
                            last = e.matmul(pb.t[0:n, :], lhsT=lhsT, rhs=wo_v[:, k, hf * 512:(hf + 1) * 512],
                                            start=(k == 0), stop=(k == 15))
                        return last
                    P.op("pe", mo, R=[ycat_ssd, ycat_sc] + wo_b, W=[pb])
                    P.op("dve", lambda e, pb=pb, hf=hf: e.tensor_tensor(
                        out=x1t.t[0:n, hf * 512:(hf + 1) * 512], in0=pb.t[0:n, :], in1=xt.t[0:n, hf * 512:(hf + 1) * 512],
                        op=ALU.add), R=[pb, xt], W=[x1t])
                P.dma("sp", [(x1_rows[i].t, x1t.t[0:n, :])], x1_dd[i % 3], R=[x1t], W=[x1_rows[i]])

            def o1(i):
                n = 128 if i < NCH else NS
                rms_scale_transpose(i, x1_b[i % 3], n, c_nffn, ss_b[i % 2], rs_b[i % 2], xnb_b[i % 2], None, None, None,
                                    do_tr=False)

            def o2(i):
                n = 128 if i < NCH else NS
                transpose_to(i, n, xnb_b[i % 2], pT[i % 2], xn2T, xn2T_b)
            run_pipeline(NCH + 1, [o0, o1, o2])
            P.barrier()
        S1b.close()
        S1.close()
        if "xn2T" in tapo:
            P.dma("sp", [(tapo["xn2T"], xn2T[:])], d_out, R=xn2T_b)
        if stop_after <= 5:
            P.finish()
            return nc

        S4 = es.enter_context(ExitStack())
        hff = sb("hff", [128, NFT, TT], BF16, S4)
        arena5 = S4.enter_context(nc.sbuf_tensor("arena5", [128, 11264], F32))
        wd_v = arena5[:, :].bitcast(BF16).rearrange("p (k c) -> p k c", k=NFT)
        with ExitStack() as ph:
            banks = [ps("p5b%d" % j, [128, 512], F32, ph) for j in range(8)]
            bstate = {"n": 0}

            def nbank():
                b = banks[bstate["n"] % len(banks)]
                bstate["n"] += 1
                return b
            load_wtile = make_wring(ph, "c", 6)
            raw_b = [Buf(arena5[:, 0:2050]), Buf(arena5[:, 2050:4100])]
            acc_b = [Buf(arena5[:, 4100:6148]), Buf(arena5[:, 6148:8196])]
            sg_b = [Buf(arena5[:, 8196:9220].bitcast(BF16)), Buf(arena5[:, 9220:10244].bitcast(BF16))]
            for j in range(2):
                P.op("pool", lambda e, j=j: e.memset(raw_b[j].t[:, 0:2], 0.0), W=[raw_b[j]])
            sup = sb("sup", [128, NFT, NS], F32, ph)
            stg_b = [sb("p5st%d" % j, [2, 128], F32, ph) for j in range(2)]
            stg_d = [P.dsem() for _ in range(2)]
            def f0(m):
                raw = raw_b[m % 2]
                wt = load_wtile(w_ffn_t[2 * m])

                def g_consume(nb, pb, c0, n):
                    if nb < 4:
                        P.op("act", lambda e: e.activation(out=raw.t[:, 2 + c0:2 + c0 + n], in_=pb.t[:, 0:n],
                                                           func=AF.Copy), R=[pb], W=[raw])
                    else:
                        P.op("act", lambda e: e.activation(out=raws_f.t[:, m, 2, :], in_=pb.t[:, 0:NS], func=AF.Copy),
                             R=[pb], W=[raws_f])
                fm_matmul(wt, xn2T, xn2T_b, banks, bstate, g_consume)

            def f1(m):
                raw, acc = raw_b[m % 2], acc_b[m % 2]
                P.op("act", lambda e: e.activation(
                    out=acc.t[:], in_=raw.t[:, 0:T], func=AF.Identity, bias=c_cbf.t[:, m:m + 1],
                    scale=c_cwf.t[:, m, 0:1]), R=[raw, c_cwf, c_cbf], W=[acc])
                for k in range(1, 3):
                    P.op("dve", lambda e, k=k: e.scalar_tensor_tensor(
                        out=acc.t[:], in0=raw.t[:, k:k + T], scalar=c_cwf.t[:, m, k:k + 1], in1=acc.t[:],
                        op0=ALU.mult, op1=ALU.add), R=[raw, acc, c_cwf], W=[acc])
                pst = nbank()
                stg = stg_b[m % 2]
                P.op("pe", lambda e: e.transpose(pst.t[0:2, 0:128], raw.t[:, T:T + 2], ident_f.t[:]),
                     R=[raw, ident_f], W=[pst])
                P.op("act", lambda e: e.activation(out=stg.t[:], in_=pst.t[0:2, 0:128], func=AF.Copy),
                     R=[pst], W=[stg])
                P.dma("sp", [(ffn_p[:, m * 128:(m + 1) * 128], stg.t[:])], stg_d[m % 2], R=[stg])

            def f2(m):
                acc, sg = acc_b[m % 2], sg_b[m % 2]
                P.op("act", lambda e: e.activation(out=sg.t[:], in_=acc.t[:], func=AF.Silu), R=[acc], W=[sg])
                wt = load_wtile(w_ffn_t[2 * m + 1])

                def u_consume(nb, pb, c0, n):
                    if nb < 4:
                        P.op("dve", lambda e: e.tensor_tensor(out=hff.t[:, m, c0:c0 + n], in0=pb.t[:, 0:n],
                                                              in1=sg.t[:, c0:c0 + n], op=ALU.mult),
                             R=[pb, sg], W=[hff])
                    else:
                        P.op("act", lambda e: e.activation(out=sup.t[:, m, :], in_=pb.t[:, 0:NS], func=AF.Copy),
                             R=[pb], W=[sup])
                fm_matmul(wt, xn2T, xn2T_b, banks, bstate, u_consume)
            run_pipeline(NFT, [f0, f1, f2])
            wd_b = []
            for qi, (k0, k1) in enumerate(((0, 4), (4, 8), (8, 12), (12, 16), (16, 20), (20, 22))):
                wd_b.append(Buf(wd_v))
                P.dma("act", [(wd_v[:, k0:k1, :], wd_bf[:, k0:k1, :])], P.dsem(), R=[wd_bf_b],
                      W=[wd_b[-1]] + (raw_b + acc_b + sg_b if qi == 0 else []))
            sacc = sb("sacc3", [128, NFT, NS], F32, ph)
            stmp = sb("stmp3", [128, NFT, NS], F32, ph)
            for k in range(3):
                wk = c_cwf.t[:, :, k:k + 1].to_broadcast([128, NFT, NS])
                if k == 0:
                    P.op("dve", lambda e, wk=wk: e.tensor_tensor(out=sacc.t[:], in0=raws_f.t[:, :, 0, :], in1=wk,
                                                                 op=ALU.mult), R=[raws_f, c_cwf], W=[sacc])
                else:
                    P.op("dve", lambda e, wk=wk, k=k: e.tensor_tensor(out=stmp.t[:], in0=raws_f.t[:, :, k, :], in1=wk,
                                                                      op=ALU.mult), R=[raws_f, c_cwf], W=[stmp])
                    P.op("dve", lambda e: e.tensor_tensor(out=sacc.t[:], in0=sacc.t[:], in1=stmp.t[:], op=ALU.add),
                         R=[sacc, stmp], W=[sacc])
            P.op("dve", lambda e: e.tensor_tensor(out=sacc.t[:], in0=sacc.t[:],
                                                  in1=c_cbf.t[:].unsqueeze(2).to_broadcast([128, NFT, NS]),
                                                  op=ALU.add), R=[sacc, c_cbf], W=[sacc])
            P.op("act", lambda e: e.activation(out=stmp.t[:], in_=sacc.t[:], func=AF.Silu), R=[sacc], W=[stmp])
            P.op("dve", lambda e: e.tensor_tensor(out=hff.t[:, :, T:TT], in0=stmp.t[:], in1=sup.t[:], op=ALU.mult),
                 R=[stmp, sup], W=[hff])
            stg2_b = [sb("p5sf%d" % j, [NS, 512], F32, ph) for j in range(2)]
            stg2_d = [P.dsem() for _ in range(2)]
            for q in range(6):
                nt = 4 if q < 5 else 2
                pb = nbank()
                stg = stg2_b[q % 2]

                def trf(e, pb=pb, q=q, nt=nt):
                    last = None
                    for jj in range(nt):
                        last = e.transpose(pb.t[0:NS, jj * 128:(jj + 1) * 128], raws_f.t[:, q * 4 + jj, 2, :],
                                           ident_f.t[:])
                    return last
                P.op("pe", trf, R=[raws_f, ident_f], W=[pb])
                P.op("act", lambda e, pb=pb, stg=stg, nt=nt: e.activation(out=stg.t[:, 0:nt * 128],
                                                                          in_=pb.t[0:NS, 0:nt * 128], func=AF.Copy),
                     R=[pb], W=[stg])
                P.dma("sp", [(ffn_s[:, 1, q * 512:q * 512 + nt * 128], stg.t[:, 0:nt * 128])], stg2_d[q % 2], R=[stg])
            P.barrier()
        SR.close()
        if stop_after <= 6:
            P.finish()
            return nc

        with ExitStack() as ph:
            pb6 = [ps("p6b%d" % j, [128, 512], F32, ph) for j in range(4)]
            c_nfin = sb("c_nfin", [128, 1024], F32, ph)
            P.dma("sp", [(c_nfin.t[:], nfin_bc)], d_cn, W=[c_nfin])
            xt_b = [sb("p6x%d" % j, [128, 1024], F32, ph) for j in range(3)]
            xt_d = [P.dsem() for _ in range(3)]
            x2_b = [sb("p6y%d" % j, [128, 1024], F32, ph) for j in range(2)]
            yo_b = [sb("p6o%d" % j, [128, 1024], F32, ph) for j in range(2)]
            yo_d = [P.dsem() for _ in range(2)]
            jk_b = [sb("p6j%d" % j, [128, 1024], BF16, ph) for j in range(2)]
            ss_b = [sb("p6s%d" % j, [128, 1], F32, ph) for j in range(2)]
            rs_b = [sb("p6r%d" % j, [128, 1], F32, ph) for j in range(2)]
            for i in range(NCH + 1):
                n = 128 if i < NCH else NS
                c0 = i * 128
                xt, x2t, yo = xt_b[i % 3], x2_b[i % 2], yo_b[i % 2]
                P.dma("sp", [(xt.t[0:n, :], x1_rows[i].t)], xt_d[i % 3], R=[x1_rows[i]], W=[xt])
                for hf in range(2):
                    pb = pb6[(2 * i + hf) % 4]

                    def md(e, pb=pb, hf=hf, c0=c0, n=n):
                        last = None
                        for k in range(NFT):
                            last = e.matmul(pb.t[0:n, :], lhsT=hff.t[:, k, c0:c0 + n],
                                            rhs=wd_v[:, k, hf * 512:(hf + 1) * 512], start=(k == 0), stop=(k == NFT - 1))
                        return last
                    P.op("pe", md, R=[hff] + wd_b, W=[pb])
                    P.op("dve", lambda e, pb=pb, hf=hf, n=n, xt=xt, x2t=x2t: e.tensor_tensor(
                        out=x2t.t[0:n, hf * 512:(hf + 1) * 512], in0=pb.t[0:n, :], in1=xt.t[0:n, hf * 512:(hf + 1) * 512],
                        op=ALU.add), R=[pb, xt], W=[x2t])
                rms_scale_transpose(i, x2t, n, c_nfin, ss_b[i % 2], rs_b[i % 2], jk_b[i % 2], None, None, None, act_out=yo)
                dst = y_p[c0:c0 + 128, :] if i < NCH else y_s
                P.dma("sp", [(dst, yo.t[0:n, :])], yo_d[i % 2], R=[yo])
            P.barrier()
        P.finish()
    return nc


def _prep_shared(inp):
    f = np.float32
    w_in = np.asarray(inp["w_in"][0], f)
    cols = [O_XBC + 128 * m for m in range(12)]
    for m in range(8):
        cols += [O_GC + 128 * m, O_H + 128 * m, O_GB + 128 * m]

    def ftile(w, c0, n=128):
        return np.ascontiguousarray(w[:, c0:c0 + n].reshape(8, 128, n).transpose(1, 0, 2))
    sh = {}
    sh["w_in_t"] = np.stack([ftile(w_in, c) for c in cols])
    sh["w_dt"] = ftile(w_in, O_DT, 16)
    sh["w_z"] = ftile(w_in, O_Z, 1024)
    sh["w_out_t"] = np.ascontiguousarray(np.asarray(inp["w_out"][0], f).reshape(16, 128, 1024).transpose(1, 0, 2))
    w_ffn = np.asarray(inp["w_ffn_in"][0], f)
    fcols = []
    for m in range(NFT):
        fcols += [128 * m, DFF + 128 * m]
    sh["w_ffn_t"] = np.stack([ftile(w_ffn, c) for c in fcols])
    sh["w_down_t"] = np.ascontiguousarray(np.asarray(inp["w_down"][0], f).reshape(NFT, 128, 1024).transpose(1, 0, 2))

    def bc(v):
        return np.ascontiguousarray(np.broadcast_to(np.asarray(v, f).reshape(1, -1), (128, np.asarray(v).size)))
    sh["nmix_bc"] = bc(inp["norm_mix_w"][0])
    sh["nffn_bc"] = bc(inp["norm_ffn_w"][0])
    sh["nfin_bc"] = bc(inp["norm_final_w"])
    sh["nssd_bc"] = bc(inp["ssd_norm_w"][0])
    sh["d_bc"] = bc(np.repeat(np.asarray(inp["ssd_d"][0], f), 64))
    sh["alog_bc"] = bc(inp["ssd_a_log"][0])
    sh["dtb"] = bc(inp["ssd_dt_bias"][0])

    def fmaj(v, nt):
        v = np.asarray(v, f)
        lead = v.shape[:-1]
        v = v.reshape(lead + (nt, 128))
        v = np.moveaxis(v, -1, 0)
        v = np.moveaxis(v, -1, 1)
        return np.ascontiguousarray(v)
    sh["cw_xbc"] = fmaj(inp["ssd_conv_w"][0], 12)
    sh["cb_xbc"] = fmaj(inp["ssd_conv_b"][0], 12)
    sh["cw_sc"] = fmaj(inp["sc_conv_w"][0], 8)
    sh["cw_ffn"] = fmaj(inp["ffn_conv_w"][0], NFT)
    sh["cb_ffn"] = fmaj(inp["ffn_conv_b"][0], NFT)
    sh["_fmaj"] = fmaj
    return sh


def _in_maps(inp):
    f = np.float32
    sh = _prep_shared(inp)
    fmaj = sh.pop("_fmaj")
    maps = []

    def padr(a):
        a = np.transpose(a, (0, 1, 3, 2))
        z = np.zeros(a.shape[:2] + (1, a.shape[3]), a.dtype)
        return np.ascontiguousarray(np.concatenate([a, z], axis=2))
    for c in range(8):
        bs = slice(NS * c, NS * (c + 1))
        d = dict(sh)
        d["xp"] = np.ascontiguousarray(inp["x_prompt"][c], f)
        d["xs"] = np.ascontiguousarray(inp["x_sample"][bs, 0, :], f)
        d["st_ssm"] = np.ascontiguousarray(np.asarray(inp["state_ssm"][0, bs], f).reshape(128, 16384))
        sx = np.asarray(inp["state_ssd_conv"][0, bs], f)
        d["st_xbc_n"] = np.ascontiguousarray(sx)
        d["st_xbc_f"] = padr(fmaj(sx, 12))
        ss = np.asarray(inp["state_short_conv"][0, bs], f)
        d["st_sc_n"] = np.ascontiguousarray(ss)
        d["st_sc_f"] = padr(fmaj(ss, 8))
        sf = np.asarray(inp["state_ffn_conv"][0, bs], f)
        d["st_ffn_n"] = np.ascontiguousarray(sf)
        d["st_ffn_f"] = padr(fmaj(sf, NFT))
        maps.append(d)
    return maps


def kernel(**inp):
    nc = build()
    maps = _in_maps(inp)
    res = run_bass_kernel_spmd(nc, maps, core_ids=list(range(8)))
    R = res.results
    f = np.float32
    y_prompt = np.stack([R[c]["y_p"] for c in range(8)]).astype(f)
    y_sample = np.concatenate([R[c]["y_s"] for c in range(8)]).reshape(128, 1, 1024).astype(f)
    ssm_p = np.stack([R[c]["ssm_p"].reshape(16, 64, 128) for c in range(8)])[None].astype(f)
    xbc_p = np.stack([R[c]["xbc_p"] for c in range(8)])[None].astype(f)
    sc_p = np.stack([R[c]["sc_p"] for c in range(8)])[None].astype(f)
    ffn_p = np.stack([R[c]["ffn_p"] for c in range(8)])[None].astype(f)
    ssm_s = np.concatenate([R[c]["ssm_s"].reshape(16, 16, 64, 128) for c in range(8)])[None].astype(f)
    xbc_s = np.concatenate([R[c]["xbc_s"] for c in range(8)])[None].astype(f)
    sc_s = np.concatenate([R[c]["sc_s"] for c in range(8)])[None].astype(f)
    ffn_s = np.concatenate([R[c]["ffn_s"] for c in range(8)])[None].astype(f)
    return (y_prompt, y_sample, ssm_p, xbc_p, sc_p, ffn_p, ssm_s, xbc_s, sc_s, ffn_s)
```

```python
import numpy as np
from contextlib import ExitStack
import concourse.bass as bass
import concourse.mybir as mybir
from concourse.bass_utils import run_bass_kernel_spmd

F32 = mybir.dt.float32
BF16 = mybir.dt.bfloat16
ALU = mybir.AluOpType
AF = mybir.ActivationFunctionType
AX = mybir.AxisListType

T = 2048
NCH = 16
NS = 16
TT = T + NS
EPS = 1e-5
DFF = 2816
NFT = 22

O_Z, O_XBC, O_DT, O_GB, O_GC, O_H = 0, 1024, 2560, 2576, 3600, 4624


class Buf:
    def __init__(self, t):
        self.t = t
        self.w = {}
        self.r = {}


class Alias:
    def __init__(self, parent, view):
        self.p = parent
        self.t = view

    @property
    def w(self):
        return self.p.w

    @w.setter
    def w(self, v):
        self.p.w = v

    @property
    def r(self):
        return self.p.r

    @r.setter
    def r(self, v):
        self.p.r = v


class Prog:
    def __init__(self, nc, es):
        self.nc = nc
        self.es = es
        self.eng = {"pe": nc.tensor, "act": nc.scalar, "dve": nc.vector, "pool": nc.gpsimd, "sp": nc.sync}
        self.semh = {}
        self.cnt = {}
        for k in ["pe", "act", "dve", "pool"]:
            self.semh[k] = es.enter_context(nc.semaphore("s_" + k))
            self.cnt[k] = 0
        self.seen = {k: {} for k in self.eng}
        self.nd = 0

    def dsem(self, name=None):
        self.nd += 1
        key = "d%d" % self.nd
        self.semh[key] = self.es.enter_context(self.nc.semaphore("s_" + key))
        self.cnt[key] = 0
        return key

    def _deps(self, e, R, W):
        deps = {}

        def add(d, own_ok):
            for k, v in d.items():
                if k == e and not own_ok:
                    continue
                if deps.get(k, 0) < v:
                    deps[k] = v

        for b in R:
            add(b.w, True)
        for b in W:
            add(b.w, True)
            add(b.r, True)
        if e == "pe":
            deps.pop("pe", None)
        return deps

    def _wait(self, e, deps):
        for k, v in deps.items():
            if self.seen[e].get(k, 0) >= v:
                continue
            self.eng[e].wait_ge(self.semh[k], v)
            self.seen[e][k] = v

    def _commit(self, key, val, R, W):
        for b in R:
            if b.r.get(key, 0) < val:
                b.r[key] = val
        for b in W:
            b.w = {key: val}
            b.r = {}

    def op(self, e, fn, R=(), W=()):
        self._wait(e, self._deps(e, R, W))
        ins = fn(self.eng[e])
        self.cnt[e] += 1
        ins.then_inc(self.semh[e], 1)
        self._commit(e, self.cnt[e], R, W)

    def dma(self, q, pairs, key, R=(), W=(), **kw):
        self._wait(q, self._deps(q, R, W))
        for (o, i) in pairs:
            self.eng[q].dma_start(out=o, in_=i, **kw).then_inc(self.semh[key], 16)
            self.cnt[key] += 16
        self._commit(key, self.cnt[key], R, W)

    def barrier(self):
        allv = dict(self.cnt)
        for e in self.eng:
            self._wait(e, {k: v for k, v in allv.items() if v > 0})

    def finish(self):
        self._wait("sp", {k: v for k, v in self.cnt.items() if v > 0})


def build(stop_after=99, taps=()):
    nc = bass.Bass("TRN2", target_bir_lowering=False)

    def din(name, shape, dt=F32):
        return nc.dram_tensor(name, list(shape), dt, kind="ExternalInput").ap()

    def dout(name, shape, dt=F32):
        return nc.dram_tensor(name, list(shape), dt, kind="ExternalOutput").ap()

    xp = din("xp", [T, 1024])
    xs = din("xs", [NS, 1024])
    st_ssm = din("st_ssm", [128, 16384])
    st_xbc_f = din("st_xbc_f", [128, 12, 4, NS])
    st_xbc_n = din("st_xbc_n", [NS, 3, 1536])
    st_sc_f = din("st_sc_f", [128, 8, 3, NS])
    st_sc_n = din("st_sc_n", [NS, 2, 1024])
    st_ffn_f = din("st_ffn_f", [128, NFT, 3, NS])
    st_ffn_n = din("st_ffn_n", [NS, 2, DFF])
    w_in_t = din("w_in_t", [36, 128, 8, 128])
    w_dt = din("w_dt", [128, 8, 16])
    w_z = din("w_z", [128, 8, 1024])
    w_out_t = din("w_out_t", [128, 16, 1024])
    w_ffn_t = din("w_ffn_t", [2 * NFT, 128, 8, 128])
    w_down_t = din("w_down_t", [128, NFT, 1024])
    nmix_bc = din("nmix_bc", [128, 1024])
    nffn_bc = din("nffn_bc", [128, 1024])
    nfin_bc = din("nfin_bc", [128, 1024])
    nssd_bc = din("nssd_bc", [128, 1024])
    d_bc = din("d_bc", [128, 1024])
    alog_bc = din("alog_bc", [128, 16])
    dtb = din("dtb", [128, 16])
    cw_xbc = din("cw_xbc", [128, 12, 4])
    cb_xbc = din("cb_xbc", [128, 12])
    cw_sc = din("cw_sc", [128, 8, 3])
    cw_ffn = din("cw_ffn", [128, NFT, 3])
    cb_ffn = din("cb_ffn", [128, NFT])

    y_p = dout("y_p", [T, 1024])
    y_s = dout("y_s", [NS, 1024])
    ssm_p = dout("ssm_p", [1024, 128])
    xbc_p = dout("xbc_p", [3, 1536])
    sc_p = dout("sc_p", [2, 1024])
    ffn_p = dout("ffn_p", [2, DFF])
    ssm_s = dout("ssm_s", [128, 16384])
    xbc_s = dout("xbc_s", [NS, 3, 1536])
    sc_s = dout("sc_s", [NS, 2, 1024])
    ffn_s = dout("ffn_s", [NS, 2, DFF])
    tapo = {}
    for (nm, shp, dt) in taps:
        tapo[nm] = dout("tap_" + nm, shp, dt)

    x1_d = nc.dram_tensor("x1_d", [TT, 1024], F32, kind="Internal").ap()
    wo_bf = nc.dram_tensor("wo_bf", [128, 16, 1024], BF16, kind="Internal").ap()
    wd_bf = nc.dram_tensor("wd_bf", [128, NFT, 1024], BF16, kind="Internal").ap()
    scr_d = nc.dram_tensor("scr_d", [NS, 16 + 1024 + 1024 + 1024], F32, kind="Internal").ap()
    ys_d = nc.dram_tensor("ys_d", [128, 128], F32, kind="Internal").ap()

    with ExitStack() as es:
        P = Prog(nc, es)

        def sb(name, shape, dt, stack=es):
            return Buf(stack.enter_context(nc.sbuf_tensor(name, list(shape), dt)))

        def ps(name, shape, dt, stack):
            return Buf(stack.enter_context(nc.psum_tensor(name, list(shape), dt)))

        d_const = P.dsem()
        d_out = P.dsem()
        d_pass = P.dsem()

        ident_f = sb("ident_f", [128, 128], F32)
        ident_b = sb("ident_b", [128, 128], BF16)
        tri_f = sb("tri_f", [128, 128], F32)
        lst_f = sb("lst_f", [128, 128], F32)
        ones_f = sb("ones_f", [128, 128], F32)
        mhalf = sb("mhalf", [128, 4], F32)
        P.op("pool", lambda e: e.memset(ident_f.t[:], 1.0), W=[ident_f])
        P.op("pool", lambda e: e.affine_select(out=ident_f.t[:], in_=ident_f.t[:], pattern=[[-1, 128]],
                                               compare_op=ALU.is_equal, fill=0.0, base=0, channel_multiplier=1),
             R=[ident_f], W=[ident_f])
        P.op("pool", lambda e: e.memset(tri_f.t[:], 1.0), W=[tri_f])
        P.op("pool", lambda e: e.affine_select(out=tri_f.t[:], in_=tri_f.t[:], pattern=[[1, 128]],
                                               compare_op=ALU.is_ge, fill=0.0, base=0, channel_multiplier=-1),
             R=[tri_f], W=[tri_f])
        P.op("pool", lambda e: e.memset(lst_f.t[:], 1.0), W=[lst_f])
        P.op("pool", lambda e: e.affine_select(out=lst_f.t[:], in_=lst_f.t[:], pattern=[[-1, 128]],
                                               compare_op=ALU.is_gt, fill=0.0, base=0, channel_multiplier=1),
             R=[lst_f], W=[lst_f])
        P.op("pool", lambda e: e.memset(ones_f.t[:], 1.0), W=[ones_f])
        P.op("pool", lambda e: e.memset(mhalf.t[:], -0.5), W=[mhalf])
        P.op("dve", lambda e: e.tensor_copy(ident_b.t[:], ident_f.t[:]), R=[ident_f], W=[ident_b])

        c_nssd = sb("c_nssd", [128, 1024], F32)
        c_d = sb("c_d", [128, 1024], F32)
        c_alog = sb("c_alog", [128, 16], F32)
        c_a = sb("c_a", [128, 16], F32)
        c_dtb = sb("c_dtb", [128, 16], F32)
        c_cwx = sb("c_cwx", [128, 12, 4], F32)
        c_cbx = sb("c_cbx", [128, 12], F32)
        c_cws = sb("c_cws", [128, 8, 3], F32)
        c_cwf = sb("c_cwf", [128, NFT, 3], F32)
        c_cbf = sb("c_cbf", [128, NFT], F32)
        consts = [c_nssd, c_d, c_alog, c_dtb, c_cwx, c_cbx, c_cws, c_cwf, c_cbf]
        P.dma("act", [(c_nssd.t[:], nssd_bc), (c_d.t[:], d_bc), (c_alog.t[:], alog_bc),
                     (c_dtb.t[:], dtb), (c_cwx.t[:], cw_xbc), (c_cbx.t[:], cb_xbc), (c_cws.t[:], cw_sc),
                     (c_cwf.t[:], cw_ffn), (c_cbf.t[:], cb_ffn)], d_const, W=consts)

        raws_x = sb("raws_x", [128, 12, 4, NS], F32)
        raws_s = sb("raws_s", [128, 8, 3, NS], F32)
        raws_f = sb("raws_f", [128, NFT, 3, NS], F32)
        d_raws = P.dsem()
        d_cn = P.dsem()
        P.dma("act", [(raws_x.t[:], st_xbc_f), (raws_s.t[:], st_sc_f), (raws_f.t[:], st_ffn_f)], d_raws,
              W=[raws_x, raws_s, raws_f])
        P.dma("act", [(xbc_s[:, 0:2, :], st_xbc_n[:, 1:3, :]), (sc_s[:, 0:1, :], st_sc_n[:, 1:2, :]),
                     (ffn_s[:, 0:1, :], st_ffn_n[:, 1:2, :])], d_pass)

        wo_bf_b = Buf(wo_bf)
        wd_bf_b = Buf(wd_bf)
        d_pc = P.dsem()

        P.op("act", lambda e: e.activation(out=c_a.t[:], in_=c_alog.t[:], func=AF.Exp), R=[c_alog], W=[c_a])
        P.op("dve", lambda e: e.tensor_scalar(out=c_a.t[:], in0=c_a.t[:], scalar1=-1.0, scalar2=None, op0=ALU.mult),
             R=[c_a], W=[c_a])

        S1 = es.enter_context(ExitStack())
        ycat_ssd = sb("ycat_ssd", [128, 8, TT], BF16, S1)
        us_tok = sb("us_tok", [NS, 1024], BF16, S1)
        xs_T = sb("xs_T", [128, 12, NS], F32, S1)
        dt_s = sb("dt_s", [NS, 16], F32, S1)
        S2 = es.enter_context(ExitStack())
        xnT = S2.enter_context(nc.sbuf_tensor("xnT", [128, 8, TT], BF16, side="right"))
        xnT_b = [Buf(xnT) for _ in range(5)]

        def run_pipeline(n_items, stages):
            st2 = [(st if isinstance(st, tuple) else (st, k)) for k, st in enumerate(stages)]
            maxlag = max(l for _, l in st2)
            for it in range(n_items + maxlag):
                for fn, lag in st2:
                    m = it - lag
                    if 0 <= m < n_items:
                        fn(m)

        def blk_of_tile(i):
            return 4 if i == NCH else i // 4

        def rms_scale_transpose(i, xt, nrows, wbc, ss, rstd, xnb, pT, dstT, dst_bufs, act_out=None, do_tr=True):
            n = nrows
            P.op("act", lambda e: e.activation(out=xnb.t[0:n, :], in_=xt.t[0:n, :], func=AF.Square,
                                               accum_out=ss.t[0:n, 0:1]), R=[xt], W=[xnb, ss])
            P.op("pool", lambda e: e.tensor_scalar(out=rstd.t[0:n, 0:1], in0=ss.t[0:n, 0:1], scalar1=1.0 / 1024,
                                                   scalar2=EPS, op0=ALU.mult, op1=ALU.add), R=[ss], W=[rstd])
            P.op("pool", lambda e: e.tensor_tensor(out=rstd.t[0:n, 0:1], in0=rstd.t[0:n, 0:1], in1=mhalf.t[0:n, 0:1],
                                                   op=ALU.pow), R=[rstd, mhalf], W=[rstd])
            if act_out is not None:
                P.op("dve", lambda e: e.scalar_tensor_tensor(out=act_out.t[0:n, :], in0=xt.t[0:n, :],
                                                             scalar=rstd.t[0:n, 0:1], in1=wbc.t[0:n, :],
                                                             op0=ALU.mult, op1=ALU.mult),
                     R=[xt, rstd, wbc], W=[act_out])
                return
            P.op("dve", lambda e: e.scalar_tensor_tensor(out=xnb.t[0:n, :], in0=xt.t[0:n, :], scalar=rstd.t[0:n, 0:1],
                                                         in1=wbc.t[0:n, :], op0=ALU.mult, op1=ALU.mult),
                 R=[xt, rstd, wbc], W=[xnb])
            if do_tr:
                transpose_to(i, n, xnb, pT, dstT, dst_bufs)

        def transpose_to(i, n, xnb, pT, dstT, dst_bufs, copy_eng="act"):
            pv = pT.t[:].bitcast(BF16)

            def tr(e):
                last = None
                for k in range(8):
                    last = e.transpose(pv[:, k * 128:k * 128 + n], xnb.t[0:n, k * 128:(k + 1) * 128],
                                       ident_b.t[0:n, 0:n])
                return last
            P.op("pe", tr, R=[xnb, ident_b], W=[pT])
            c0 = i * 128
            if copy_eng == "act":
                P.op("act", lambda e: e.activation(out=dstT[:, :, c0:c0 + n],
                                                   in_=pv.rearrange("p (k t) -> p k t", k=8)[:, :, 0:n],
                                                   func=AF.Copy), R=[pT], W=[dst_bufs[blk_of_tile(i)]])
            else:
                P.op("dve", lambda e: e.tensor_copy(dstT[:, :, c0:c0 + n],
                                                    pv.rearrange("p (k t) -> p k t", k=8)[:, :, 0:n]),
                     R=[pT], W=[dst_bufs[blk_of_tile(i)]])

        with ExitStack() as ph:
            pT = [ps("p1T%d" % j, [128, 512], F32, ph) for j in range(2)]
            c_nmix = sb("c_nmix", [128, 1024], F32, ph)
            P.dma("sp", [(c_nmix.t[:], nmix_bc)], d_cn, W=[c_nmix])
            xt_b = [sb("p1x%d" % j, [128, 1024], F32, ph) for j in range(3)]
            xt_d = [P.dsem() for _ in range(3)]
            xnb_b = [sb("p1n%d" % j, [128, 1024], BF16, ph) for j in range(3)]
            ss_b = [sb("p1s%d" % j, [128, 1], F32, ph) for j in range(2)]
            rs_b = [sb("p1r%d" % j, [128, 1], F32, ph) for j in range(2)]
            def n0(i):
                n = 128 if i < NCH else NS
                src = xp[i * 128:(i + 1) * 128, :] if i < NCH else xs
                xt = xt_b[i % 3]
                P.dma("sp", [(xt.t[0:n, :], src)], xt_d[i % 3], W=[xt])
                rms_scale_transpose(i, xt, n, c_nmix, ss_b[i % 2], rs_b[i % 2], xnb_b[i % 3], None, None, None,
                                    do_tr=False)

            def n1(i):
                n = 128 if i < NCH else NS
                transpose_to(i, n, xnb_b[i % 3], pT[i % 2], xnT, xnT_b, copy_eng="dve")
            run_pipeline(NCH + 1, [n0, n1])
            P.barrier()
        if "xnT" in tapo:
            P.dma("sp", [(tapo["xnT"], xnT[:])], d_out, R=xnT_b)
        if stop_after <= 1:
            P.finish()
            return nc

        NW = 4
        wr_d = [P.dsem() for _ in range(NW)]

        def make_wring(stack, tag):
            wr_b = [sb("wr%s%d" % (tag, j), [128, 8, 128], BF16, stack) for j in range(NW)]
            wstate = {"n": 0}

            def load_wtile(src_ap):
                j = wstate["n"] % NW
                wstate["n"] += 1
                P.dma("pool", [(wr_b[j].t[:], src_ap)], wr_d[j], W=[wr_b[j]])
                return wr_b[j]
            return load_wtile

        def fm_matmul(wt, xT, xT_bufs, banks, bstate, consume, mcols=128):
            for nb in range(5):
                pb = banks[bstate["n"] % len(banks)]
                bstate["n"] += 1
                c0, n = (nb * 512, 512) if nb < 4 else (T, NS)

                def mm(e, pb=pb, c0=c0, n=n):
                    last = None
                    for k in range(8):
                        last = e.matmul(pb.t[0:mcols, 0:n], lhsT=wt.t[:, k, 0:mcols], rhs=xT[:, k, c0:c0 + n],
                                        start=(k == 0), stop=(k == 7))
                    return last
                P.op("pe", mm, R=[wt, xT_bufs[nb]], W=[pb])
                consume(nb, pb, c0, n)

        S3 = es.enter_context(ExitStack())
        x_tok = sb("x_tok", [128, NCH, 1024], BF16, S3)
        BT = sb("BT", [128, 2, T], BF16, S3)
        CT = sb("CT", [128, 2, T], BF16, S3)
        B_tok = sb("B_tok", [128, NCH, 256], BF16, S3)
        dt_tok = sb("dt_tok", [128, NCH, 16], F32, S3)
        da_tok = sb("da_tok", [128, NCH, 16], F32, S3)
        wz_v = S3.enter_context(nc.sbuf_tensor("wz", [128, 8, 1024], BF16))
        wz_b = [Buf(wz_v) for _ in range(4)]

        with ExitStack() as ph:
            banks = [ps("p2b%d" % j, [128, 512], F32, ph) for j in range(6)]
            pTr = ps("p2T", [128, 1024], F32, ph)
            bstate = {"n": 0}
            load_wtile = make_wring(ph, "a")
            stx_b = [sb("p2st%d" % j, [3, 128], F32, ph) for j in range(2)]
            stx_d = [P.dsem() for _ in range(2)]
            wdt = sb("wdt", [128, 8, 16], BF16, ph)
            d_wdt = P.dsem()
            P.dma("pool", [(wdt.t[:], w_dt)], d_wdt, W=[wdt])
            pdt = banks[bstate["n"] % 6]
            bstate["n"] += 1
            pds = banks[bstate["n"] % 6]
            bstate["n"] += 1

            def mmdt(e):
                last = None
                for c in range(NCH):
                    for k in range(8):
                        last = e.matmul(pdt.t[:, c * 16:(c + 1) * 16], lhsT=xnT[:, k, c * 128:(c + 1) * 128],
                                        rhs=wdt.t[:, k, :], start=(k == 0), stop=(k == 7))
                return last
            P.op("pe", mmdt, R=[wdt] + xnT_b[0:4], W=[pdt])

            def mmds(e):
                last = None
                for k in range(8):
                    last = e.matmul(pds.t[0:NS, 0:16], lhsT=xnT[:, k, T:TT], rhs=wdt.t[:, k, :],
                                    start=(k == 0), stop=(k == 7))
                return last
            P.op("pe", mmds, R=[wdt, xnT_b[4]], W=[pds])
            sp_a = sb("sp_a", [128, NCH * 16], F32, ph)
            sp_s = sb("sp_s", [NS, 16], F32, ph)
            dtf = dt_tok.t[:].rearrange("p c h -> p (c h)")
            P.op("dve", lambda e: e.tensor_tensor(out=dt_tok.t[:], in0=pdt.t[:, 0:256].rearrange("p (c h) -> p c h", c=NCH),
                                                  in1=c_dtb.t[:].unsqueeze(1).to_broadcast([128, NCH, 16]), op=ALU.add),
                 R=[pdt, c_dtb], W=[dt_tok])
            P.op("dve", lambda e: e.tensor_tensor(out=dt_s.t[:], in0=pds.t[0:NS, 0:16], in1=c_dtb.t[0:NS, :], op=ALU.add),
                 R=[pds, c_dtb], W=[dt_s])
            for (tt, spb) in ((dt_tok, sp_a), (dt_s, sp_s)):
                tv = dtf if tt is dt_tok else dt_s.t[:]
                P.op("act", lambda e, tv=tv, spb=spb: e.activation(out=spb.t[:], in_=tv, func=AF.Abs), R=[tt], W=[spb])
                P.op("act", lambda e, spb=spb: e.activation(out=spb.t[:], in_=spb.t[:], func=AF.Exp, scale=-1.0),
                     R=[spb], W=[spb])
                P.op("act", lambda e, spb=spb: e.activation(out=spb.t[:], in_=spb.t[:], func=AF.Ln, bias=1.0, scale=1.0),
                     R=[spb], W=[spb])
                P.op("dve", lambda e, tv=tv, spb=spb: e.scalar_tensor_tensor(out=tv, in0=tv, scalar=0.0, in1=spb.t[:],
                                                                             op0=ALU.max, op1=ALU.add),
                     R=[tt, spb], W=[tt])
            P.op("dve", lambda e: e.tensor_tensor(out=da_tok.t[:], in0=dt_tok.t[:],
                                                  in1=c_a.t[:].unsqueeze(1).to_broadcast([128, NCH, 16]),
                                                  op=ALU.mult), R=[dt_tok, c_a], W=[da_tok])

            raw_b = [sb("p2raw%d" % j, [128, 3 + T], F32, ph) for j in range(2)]
            acc_b = [sb("p2acc%d" % j, [128, T], F32, ph) for j in range(2)]
            sil_b = [Buf(acc_b[j].t) for j in range(2)]
            for j in range(2):
                sil_b[j] = acc_b[j]
            for j in range(2):
                P.op("pool", lambda e, j=j: e.memset(raw_b[j].t[:, 0:3], 0.0), W=[raw_b[j]])
            pv = pTr.t[:].bitcast(BF16)

            def a_stage(m):
                wt = load_wtile(w_in_t[m])
                raw = raw_b[m % 2]

                def xbc_consume(nb, pb, c0, n):
                    if nb < 4:
                        P.op("act", lambda e: e.activation(out=raw.t[:, 3 + c0:3 + c0 + n], in_=pb.t[:, 0:n],
                                                           func=AF.Copy), R=[pb], W=[raw])
                    else:
                        P.op("act", lambda e: e.activation(out=raws_x.t[:, m, 3, :], in_=pb.t[:, 0:NS],
                                                           func=AF.Copy), R=[pb], W=[raws_x])
                fm_matmul(wt, xnT, xnT_b, banks, bstate, xbc_consume)

            def b1_stage(m):
                raw, acc = raw_b[m % 2], acc_b[m % 2]
                P.op("act", lambda e: e.activation(
                    out=acc.t[:], in_=raw.t[:, 0:T], func=AF.Identity, bias=c_cbx.t[:, m:m + 1],
                    scale=c_cwx.t[:, m, 0:1]), R=[raw, c_cwx, c_cbx], W=[acc])
                for k in range(1, 4):
                    P.op("dve", lambda e, k=k: e.scalar_tensor_tensor(
                        out=acc.t[:], in0=raw.t[:, k:k + T], scalar=c_cwx.t[:, m, k:k + 1], in1=acc.t[:],
                        op0=ALU.mult, op1=ALU.add), R=[raw, acc, c_cwx], W=[acc])
                pst = banks[bstate["n"] % 6]
                bstate["n"] += 1
                P.op("pe", lambda e: e.transpose(pst.t[0:3, 0:128], raw.t[:, T:T + 3], ident_f.t[:]),
                     R=[raw, ident_f], W=[pst])
                stx = stx_b[m % 2]
                P.op("act", lambda e: e.activation(out=stx.t[:], in_=pst.t[0:3, 0:128], func=AF.Copy),
                     R=[pst], W=[stx])
                P.dma("sp", [(xbc_p[:, m * 128:(m + 1) * 128], stx.t[:])], stx_d[m % 2], R=[stx])

            def dst_of(m):
                if m < 8:
                    return acc_b[m % 2].t[:].bitcast(BF16)[:, 0:T], acc_b[m % 2]
                if m < 10:
                    return BT.t[:, m - 8, :], BT
                return CT.t[:, m - 10, :], CT

            def b2_stage(m):
                acc = acc_b[m % 2]
                dst, dstb = dst_of(m)
                P.op("act", lambda e: e.activation(out=dst, in_=acc.t[:], func=AF.Silu), R=[acc], W=[dstb])

            def c_stage(m):
                if m >= 10:
                    return
                dst, dstb = dst_of(m)

                def trx(e):
                    last = None
                    for c in range(NCH):
                        last = e.transpose(pv[:, c * 128:(c + 1) * 128], dst[:, c * 128:(c + 1) * 128], ident_b.t[:])
                    return last
                P.op("pe", trx, R=[dstb, ident_b], W=[pTr])
                if m < 8:
                    P.op("dve", lambda e: e.tensor_copy(x_tok.t[:, :, m * 128:(m + 1) * 128],
                                                        pv.rearrange("p (c f) -> p c f", c=NCH)),
                         R=[pTr], W=[x_tok])
                else:
                    g = m - 8
                    P.op("dve", lambda e: e.tensor_copy(B_tok.t[:, :, g * 128:(g + 1) * 128],
                                                        pv.rearrange("p (c f) -> p c f", c=NCH)),
                         R=[pTr], W=[B_tok])
            def a_stage_w(m):
                a_stage(m)
                if 4 <= m < 8:
                    q = m - 4
                    P.dma("pool", [(wz_v[:, 2 * q:2 * q + 2, :], w_z[:, 2 * q:2 * q + 2, :])], P.dsem(), W=[wz_b[q]])
            run_pipeline(12, [(a_stage_w, 0), (c_stage, 3), (b1_stage, 1), (b2_stage, 2)])

            sbuf0 = acc_b[0]
            sacc_v = sbuf0.t[:, 0:12 * NS].rearrange("p (m t) -> p m t", m=12)
            stmp_v = sbuf0.t[:, 256:256 + 12 * NS].rearrange("p (m t) -> p m t", m=12)
            for k in range(4):
                wk = c_cwx.t[:, :, k:k + 1].to_broadcast([128, 12, NS])
                if k == 0:
                    P.op("dve", lambda e, wk=wk: e.tensor_tensor(out=sacc_v, in0=raws_x.t[:, :, 0, :], in1=wk,
                                                                 op=ALU.mult), R=[raws_x, c_cwx], W=[sbuf0])
                else:
                    P.op("dve", lambda e, wk=wk, k=k: e.tensor_tensor(out=stmp_v, in0=raws_x.t[:, :, k, :], in1=wk,
                                                                      op=ALU.mult), R=[raws_x, c_cwx, sbuf0], W=[sbuf0])
                    P.op("dve", lambda e: e.tensor_tensor(out=sacc_v, in0=sacc_v, in1=stmp_v, op=ALU.add),
                         R=[sbuf0], W=[sbuf0])
            P.op("dve", lambda e: e.tensor_tensor(out=sacc_v, in0=sacc_v,
                                                  in1=c_cbx.t[:].unsqueeze(2).to_broadcast([128, 12, NS]),
                                                  op=ALU.add), R=[sbuf0, c_cbx], W=[sbuf0])
            P.op("act", lambda e: e.activation(out=xs_T.t[:], in_=sacc_v, func=AF.Silu), R=[sbuf0], W=[xs_T])
            sxs_b = [sb("p2sx%d" % j, [NS, 512], F32, ph) for j in range(1)]
            sxs_d = [P.dsem() for _ in range(1)]
            for q in range(3):
                pst = banks[bstate["n"] % 6]
                bstate["n"] += 1
                sxs = sxs_b[0]

                def trq(e, pst=pst, q=q):
                    last = None
                    for jj in range(4):
                        last = e.transpose(pst.t[0:NS, jj * 128:(jj + 1) * 128], raws_x.t[:, q * 4 + jj, 3, :],
                                           ident_f.t[:])
                    return last
                P.op("pe", trq, R=[raws_x, ident_f], W=[pst])
                P.op("act", lambda e, pst=pst, sxs=sxs: e.activation(out=sxs.t[:], in_=pst.t[0:NS, :], func=AF.Copy),
                     R=[pst], W=[sxs])
                P.dma("sp", [(xbc_s[:, 2, q * 512:(q + 1) * 512], sxs.t[:])], sxs_d[0], R=[sxs])
            P.barrier()
        for nm, b in (("x_tok", x_tok), ("BT", BT), ("CT", CT), ("B_tok", B_tok), ("dt_tok", dt_tok),
                      ("xs_T", xs_T)):
            if nm in tapo:
                P.dma("sp", [(tapo[nm], b.t[:])], d_out, R=[b])
        if stop_after <= 2:
            P.finish()
            return nc


        with ExitStack() as ph:
            SEGa = ps("SEGa", [128, 1024], F32, ph)
            SEGb = ps("SEGb", [128, 1024], F32, ph)
            Y1 = ps("Y1", [128, 1024], F32, ph)
            Y2a = ps("Y2a", [128, 512], F32, ph)
            Y2b = ps("Y2b", [128, 512], F32, ph)
            P.dma("pool", [(wo_bf[:, 8 * q:8 * q + 8, :], w_out_t[:, 8 * q:8 * q + 8, :]) for q in range(2)], d_pc,
                  W=[wo_bf_b])
            P.dma("pool", [(wd_bf[:, 11 * q:11 * q + 11, :], w_down_t[:, 11 * q:11 * q + 11, :]) for q in range(2)],
                  P.dsem(), W=[wd_bf_b])
            hT = sb("hT", [128, 1024], F32, ph)
            hTb = sb("hTb", [128, 1024], BF16, ph)
            Rhi = sb("Rhi", [128, 2048], BF16, ph)
            lst_b = sb("lst_b", [128, 128], BF16, ph)
            P.op("dve", lambda e: e.tensor_copy(lst_b.t[:], lst_f.t[:]), R=[lst_f], W=[lst_b])
            tD_b = [sb("tD%d" % j, [128, 1024], BF16, ph) for j in range(2)]
            dec_all = sb("dec_all", [128, 3, NCH * 16], F32, ph)
            da_all = da_tok.t[:].rearrange("p c h -> p (c h)")

            def s1all(e):
                e.matmul(SEGa.t[:, 0:256], lhsT=lst_f.t[:], rhs=da_all, start=True, stop=True)
                e.matmul(SEGa.t[:, 256:512], lhsT=ones_f.t[:], rhs=da_all, start=True, stop=True)
                return e.matmul(SEGa.t[:, 512:768], lhsT=tri_f.t[:], rhs=da_all, start=True, stop=True)
            P.op("pe", s1all, R=[da_tok, lst_f, ones_f, tri_f], W=[SEGa])
            P.op("act", lambda e: e.activation(out=dec_all.t[:].rearrange("p a b -> p (a b)"), in_=SEGa.t[:, 0:768],
                                               func=AF.Exp), R=[SEGa], W=[dec_all])
            CBm = sb("CBm", [128, 256], BF16, ph)
            Mb_b = [sb("Mb%d" % j, [128, 2048], BF16, ph) for j in range(2)]
            xd_b = [sb("xd%d" % j, [128, 1024], BF16, ph) for j in range(2)]
            xdd_b = [sb("xdd%d" % j, [128, 1024], BF16, ph) for j in range(2)]
            t1 = sb("t1", [128, 1024], F32, ph)
            th = sb("th", [128, 1024], F32, ph)
            yn = sb("yn", [128, 1024], BF16, ph)
            ss2 = sb("ss2", [128, 2], F32, ph)
            rs2 = sb("rs2", [128, 2], F32, ph)
            stage_ssm = t1
            print("SSD phase sbuf remaining", nc.sbuf_bytes_remaining)

            def bc3(ap2, n_mid, n_in):
                return ap2.unsqueeze(2).to_broadcast([128, n_mid, n_in])

            tri_b = sb("tri_b", [128, 128], BF16, ph)
            P.op("dve", lambda e: e.tensor_copy(tri_b.t[:], tri_f.t[:]), R=[tri_f], W=[tri_b])

            def front(c):
                cs = slice(c * 128, (c + 1) * 128)
                Mb, xd, xdd = Mb_b[c % 2], xd_b[c % 2], xdd_b[c % 2]
                dcs = slice(c * 16, (c + 1) * 16)

                def s6(e):
                    last = None
                    for g in range(2):
                        last = e.matmul(Y2b.t[:, g * 128:(g + 1) * 128], lhsT=BT.t[:, g, cs], rhs=CT.t[:, g, cs],
                                        start=True, stop=True)
                    return last
                P.op("pe", s6, R=[BT, CT], W=[Y2b])

                def s3(e):
                    last = None
                    for h in range(16):
                        last = e.tensor_scalar(out=Rhi.t[:, h * 128:(h + 1) * 128], in0=tri_b.t[:],
                                               scalar1=da_tok.t[:, c, h:h + 1], scalar2=None, op0=ALU.mult)
                    return last
                P.op("dve", s3, R=[da_tok, tri_b], W=[Rhi])
                for hf, SEG in ((0, SEGa), (1, SEGb)):
                    def s4(e, hf=hf, SEG=SEG):
                        last = None
                        for q in range(2):
                            cols = slice(hf * 1024 + q * 512, hf * 1024 + (q + 1) * 512)
                            last = e.matmul(SEG.t[:, q * 512:(q + 1) * 512], lhsT=lst_b.t[:], rhs=Rhi.t[:, cols],
                                            start=True, stop=True)
                        return last
                    P.op("pe", s4, R=[Rhi, lst_b], W=[SEG])
                    P.op("act", lambda e, hf=hf, SEG=SEG: e.activation(out=Mb.t[:, hf * 1024:(hf + 1) * 1024],
                                                                       in_=SEG.t[:], func=AF.Exp),
                         R=[SEG], W=[Mb])
                P.op("dve", lambda e: e.tensor_tensor(
                    out=CBm.t[:].rearrange("p (g l) -> p g l", g=2),
                    in0=Y2b.t[:, 0:256].rearrange("p (g l) -> p g l", g=2),
                    in1=tri_f.t[:].unsqueeze(1).to_broadcast([128, 2, 128]), op=ALU.mult),
                    R=[Y2b, tri_f], W=[CBm])
                P.op("dve", lambda e: e.tensor_tensor(
                    out=xd.t[:].rearrange("p (h q) -> p h q", h=16),
                    in0=x_tok.t[:, c, :].rearrange("p (h q) -> p h q", h=16),
                    in1=bc3(dt_tok.t[:, c, :], 16, 64), op=ALU.mult), R=[x_tok, dt_tok], W=[xd])
                P.op("pool", lambda e: e.tensor_tensor(
                    out=xdd.t[:].rearrange("p (h q) -> p h q", h=16),
                    in0=xd.t[:].rearrange("p (h q) -> p h q", h=16),
                    in1=bc3(dec_all.t[:, 0, dcs], 16, 64), op=ALU.mult), R=[xd, dec_all], W=[xdd])

            def pre(c):
                tDc = tD_b[c % 2]
                P.op("pool", lambda e: e.tensor_tensor(out=tDc.t[:], in0=x_tok.t[:, c, :], in1=c_d.t[:],
                                                       op=ALU.mult), R=[x_tok, c_d], W=[tDc])

            def front_b(c):
                Mb = Mb_b[c % 2]
                P.op("dve", lambda e: e.tensor_tensor(
                    out=Mb.t[:].rearrange("p (g h l) -> p g h l", g=2, h=8),
                    in0=Mb.t[:].rearrange("p (g h l) -> p g h l", g=2, h=8),
                    in1=CBm.t[:].rearrange("p (g l) -> p g l", g=2).unsqueeze(2).to_broadcast([128, 2, 8, 128]),
                    op=ALU.mult), R=[Mb, CBm], W=[Mb])

            def rec(c):
                cs = slice(c * 128, (c + 1) * 128)
                xdd = xdd_b[c % 2]
                if c > 0:
                    for g, Y2 in ((0, Y2a), (1, Y2b)):
                        P.op("pe", lambda e, g=g, Y2=Y2: e.matmul(
                            Y2.t[:, 0:512], lhsT=CT.t[:, g, cs], rhs=hTb.t[:, g * 512:(g + 1) * 512],
                            start=True, stop=True), R=[CT, hTb], W=[Y2])
                    P.op("pool", lambda e: e.tensor_tensor(
                        out=hT.t[:].rearrange("p (h q) -> p h q", h=16),
                        in0=hT.t[:].rearrange("p (h q) -> p h q", h=16),
                        in1=bc3(dec_all.t[:, 1, c * 16:(c + 1) * 16], 16, 64), op=ALU.mult), R=[hT, dec_all], W=[hT])

                def s14(e):
                    last = None
                    for g in range(2):
                        last = e.matmul(SEGb.t[:, g * 512:(g + 1) * 512], lhsT=B_tok.t[:, c, g * 128:(g + 1) * 128],
                                        rhs=xdd.t[:, g * 512:(g + 1) * 512], start=True, stop=True)
                    return last
                P.op("pe", s14, R=[B_tok, xdd], W=[SEGb])
                if c > 0:
                    P.op("dve", lambda e: e.tensor_tensor(out=hT.t[:], in0=SEGb.t[:], in1=hT.t[:], op=ALU.add),
                         R=[SEGb, hT], W=[hT])
                else:
                    P.op("dve", lambda e: e.tensor_copy(hT.t[:], SEGb.t[:]), R=[SEGb], W=[hT])
                if c < NCH - 1:
                    P.op("act", lambda e: e.activation(out=hTb.t[:], in_=hT.t[:], func=AF.Copy), R=[hT], W=[hTb])

            def tail_a(c):
                cs = slice(c * 128, (c + 1) * 128)
                Mb, xd = Mb_b[c % 2], xd_b[c % 2]
                tD = tD_b[c % 2]
                if c > 0:
                    for g, Y2 in ((0, Y2a), (1, Y2b)):
                        P.op("dve", lambda e, g=g, Y2=Y2: e.tensor_tensor(
                            out=t1.t[:, g * 512:(g + 1) * 512].rearrange("p (h q) -> p h q", h=8),
                            in0=Y2.t[:, 0:512].rearrange("p (h q) -> p h q", h=8),
                            in1=bc3(dec_all.t[:, 2, c * 16 + 8 * g:c * 16 + 8 * g + 8], 8, 64), op=ALU.mult),
                            R=[Y2, dec_all], W=[t1])

                def s13(e):
                    last = None
                    for hf in range(2):
                        for k in range(8):
                            last = e.matmul(SEGa.t[:, hf * 512:(hf + 1) * 512], lhsT=xnT[:, k, cs],
                                            rhs=wz_v[:, k, hf * 512:(hf + 1) * 512], start=(k == 0), stop=(k == 7))
                    return last
                P.op("pe", s13, R=[xnT_b[c // 4]] + wz_b, W=[SEGa])
                P.op("act", lambda e: e.activation(out=th.t[:], in_=SEGa.t[:], func=AF.Silu), R=[SEGa], W=[th])

                def s11(e):
                    last = None
                    for hf in range(2):
                        e.matmul(Y1.t[:, hf * 512:(hf + 1) * 512], lhsT=ident_b.t[:], rhs=tD.t[:, hf * 512:(hf + 1) * 512],
                                 start=True, stop=False, skip_group_check=True)
                    for h in range(16):
                        last = e.matmul(Y1.t[:, h * 64:(h + 1) * 64], lhsT=Mb.t[:, h * 128:(h + 1) * 128],
                                        rhs=xd.t[:, h * 64:(h + 1) * 64], start=False, stop=(h % 8 == 7),
                                        skip_group_check=True)
                    return last
                P.op("pe", s11, R=[Mb, xd, tD, ident_b], W=[Y1])
                if c > 0:
                    P.op("dve", lambda e: e.tensor_tensor(out=t1.t[:], in0=Y1.t[:], in1=t1.t[:], op=ALU.add),
                         R=[Y1, t1], W=[t1])
                else:
                    P.op("dve", lambda e: e.tensor_copy(t1.t[:], Y1.t[:]), R=[Y1], W=[t1])
                P.op("dve", lambda e: e.tensor_tensor(out=t1.t[:], in0=t1.t[:], in1=th.t[:], op=ALU.mult),
                     R=[t1, th], W=[t1])
                for g in range(2):
                    P.op("act", lambda e, g=g: e.activation(out=th.t[:, g * 512:(g + 1) * 512],
                                                            in_=t1.t[:, g * 512:(g + 1) * 512], func=AF.Square,
                                                            accum_out=ss2.t[:, g:g + 1]), R=[t1], W=[th, ss2])
                P.op("pool", lambda e: e.tensor_scalar(out=rs2.t[:], in0=ss2.t[:], scalar1=1.0 / 512, scalar2=EPS,
                                                       op0=ALU.mult, op1=ALU.add), R=[ss2], W=[rs2])
                P.op("pool", lambda e: e.tensor_tensor(out=rs2.t[:], in0=rs2.t[:], in1=mhalf.t[:, 0:2], op=ALU.pow),
                     R=[rs2, mhalf], W=[rs2])

            def tail_b(c):
                for g in range(2):
                    P.op("dve", lambda e, g=g: e.scalar_tensor_tensor(
                        out=yn.t[:, g * 512:(g + 1) * 512], in0=t1.t[:, g * 512:(g + 1) * 512],
                        scalar=rs2.t[:, g:g + 1], in1=c_nssd.t[:, g * 512:(g + 1) * 512], op0=ALU.mult, op1=ALU.mult),
                        R=[t1, rs2, c_nssd], W=[yn])

            def post(c):
                cs = slice(c * 128, (c + 1) * 128)
                pv = Y2b.t[:].bitcast(BF16)

                def s24(e):
                    last = None
                    for j in range(8):
                        last = e.transpose(pv[:, j * 128:(j + 1) * 128], yn.t[:, j * 128:(j + 1) * 128], ident_b.t[:])
                    return last
                P.op("pe", s24, R=[yn, ident_b], W=[Y2b])
                P.op("act", lambda e: e.activation(out=ycat_ssd.t[:, :, cs],
                                                   in_=pv.rearrange("p (j t) -> p j t", j=8), func=AF.Copy),
                     R=[Y2b], W=[ycat_ssd])
            run_pipeline(NCH, [(rec, 2), (pre, 1), (tail_a, 2), (front, 1), (post, 3), (tail_b, 2), (front_b, 1)])

            def sfin(e):
                last = None
                for j in range(8):
                    last = e.transpose(SEGa.t[:, j * 128:(j + 1) * 128], hT.t[:, j * 128:(j + 1) * 128], ident_f.t[:])
                return last
            P.op("pe", sfin, R=[hT, ident_f], W=[SEGa])
            P.op("act", lambda e: e.activation(out=stage_ssm.t[:], in_=SEGa.t[:], func=AF.Copy),
                 R=[SEGa], W=[stage_ssm])
            P.dma("sp", [(ssm_p.rearrange("(j q) n -> q j n", q=128),
                          stage_ssm.t[:].rearrange("p (j n) -> p j n", j=8))], d_out, R=[stage_ssm])

            def szs(e):
                last = None
                for hf in range(2):
                    for k in range(8):
                        last = e.matmul(SEGb.t[0:NS, hf * 512:(hf + 1) * 512], lhsT=xnT[:, k, T:TT],
                                        rhs=wz_v[:, k, hf * 512:(hf + 1) * 512], start=(k == 0), stop=(k == 7))
                return last
            P.op("pe", szs, R=[xnT_b[4]] + wz_b, W=[SEGb])
            P.op("act", lambda e: e.activation(out=us_tok.t[:], in_=SEGb.t[0:NS, :], func=AF.Silu),
                 R=[SEGb], W=[us_tok])
            P.barrier()
        if "ycat_ssd" in tapo:
            P.dma("sp", [(tapo["ycat_ssd"], ycat_ssd.t[:])], d_out, R=[ycat_ssd])
        if stop_after <= 3:
            P.finish()
            return nc
        S3.close()
        S1b = es.enter_context(ExitStack())
        ycat_sc = sb("ycat_sc", [128, 8, TT], BF16, S1b)
        arena = S1b.enter_context(nc.sbuf_tensor("arena2b", [128, 10248], F32))
        wo_v = arena[:, 0:8192].bitcast(BF16).rearrange("p (k c) -> p k c", k=16)
        wo_b = [Buf(wo_v) for _ in range(4)]

        def bcm(ap2, n_mid, n_in, npart=128):
            return ap2.unsqueeze(2).to_broadcast([npart, n_mid, n_in])

        scr_dec = Buf(nc.dram_tensor("scr_dec", [NS, 16], F32, kind="Internal").ap())
        scr_dtx = Buf(nc.dram_tensor("scr_dtx", [NS, 1024], F32, kind="Internal").ap())
        scr_B = Buf(nc.dram_tensor("scr_B", [NS, 1024], F32, kind="Internal").ap())
        scr_C = Buf(nc.dram_tensor("scr_C", [NS, 1024], F32, kind="Internal").ap())
        scr_y = Buf(ys_d)
        with ExitStack() as ph:
            banks = [ps("p3b%d" % j, [128, 512], F32, ph) for j in range(7)]
            pY = ps("p3y", [128, 512], F32, ph)
            bstate = {"n": 0}

            def nbank():
                b = banks[bstate["n"] % len(banks)]
                bstate["n"] += 1
                return b
            load_wtile = make_wring(ph, "b")
            sgc = sb("sgc", [128, 8, NS], F32, ph)
            sgb = sb("sgb", [128, 8, NS], F32, ph)
            stc_b = [sb("p3st%d" % j, [2, 128], F32, ph) for j in range(2)]
            stc_d = [P.dsem() for _ in range(2)]

            xs_tok = sb("xs_tok", [NS, 1024], F32, ph)
            BC_s = sb("BC_s", [NS, 512], F32, ph)
            dec_s = sb("dec_s", [NS, 16], F32, ph)
            dtx_s = sb("dtx_s", [NS, 1024], F32, ph)
            for half in range(2):
                pb = nbank()

                def trs(e, pb=pb, half=half):
                    last = None
                    for jj in range(4):
                        last = e.transpose(pb.t[0:NS, jj * 128:(jj + 1) * 128], xs_T.t[:, half * 4 + jj, :], ident_f.t[:])
                    return last
                P.op("pe", trs, R=[xs_T, ident_f], W=[pb])
                P.op("act", lambda e, pb=pb, half=half: e.activation(out=xs_tok.t[:, half * 512:(half + 1) * 512],
                                                                     in_=pb.t[0:NS, :], func=AF.Copy),
                     R=[pb], W=[xs_tok])
            pb = nbank()

            def trbc(e, pb=pb):
                last = None
                for jj in range(4):
                    last = e.transpose(pb.t[0:NS, jj * 128:(jj + 1) * 128], xs_T.t[:, 8 + jj, :], ident_f.t[:])
                return last
            P.op("pe", trbc, R=[xs_T, ident_f], W=[pb])
            P.op("act", lambda e, pb=pb: e.activation(out=BC_s.t[:], in_=pb.t[0:NS, :], func=AF.Copy), R=[pb], W=[BC_s])
            P.op("dve", lambda e: e.tensor_tensor(out=dec_s.t[:], in0=dt_s.t[:], in1=c_a.t[0:NS, :], op=ALU.mult),
                 R=[dt_s, c_a], W=[dec_s])
            P.op("act", lambda e: e.activation(out=dec_s.t[:], in_=dec_s.t[:], func=AF.Exp), R=[dec_s], W=[dec_s])
            P.op("dve", lambda e: e.tensor_tensor(
                out=dtx_s.t[:].rearrange("p (h q) -> p h q", h=16), in0=xs_tok.t[:].rearrange("p (h q) -> p h q", h=16),
                in1=bcm(dt_s.t[:], 16, 64, NS), op=ALU.mult), R=[xs_tok, dt_s], W=[dtx_s])
            d_scr = P.dsem()
            P.dma("sp", [(scr_dec.t, dec_s.t[:]), (scr_dtx.t, dtx_s.t[:])], d_scr, R=[dec_s, dtx_s],
                  W=[scr_dec, scr_dtx])
            prs = []
            for g in range(2):
                prs.append((scr_B.t[:, g * 512:(g + 1) * 512].rearrange("b (r n) -> b r n", r=4),
                            BC_s.t[:, g * 128:(g + 1) * 128].unsqueeze(1).to_broadcast([NS, 4, 128])))
                prs.append((scr_C.t[:, g * 512:(g + 1) * 512].rearrange("b (r n) -> b r n", r=4),
                            BC_s.t[:, 256 + g * 128:256 + (g + 1) * 128].unsqueeze(1).to_broadcast([NS, 4, 128])))
            P.dma("sp", prs, d_scr, R=[BC_s], W=[scr_B, scr_C])
            dec_q = sb("dec_q", [128, 2], F32, ph)
            dtx_q = sb("dtx_q", [128, 128], F32, ph)
            B_q = sb("B_q", [128, 128], F32, ph)
            C_q = sb("C_q", [128, 128], F32, ph)
            y_q = sb("y_q", [128, 128], F32, ph)
            d_q = P.dsem()
            P.dma("sp", [(dec_q.t[:], scr_dec.t.rearrange("b (q r) -> (b q) r", q=8)),
                         (dtx_q.t[:], scr_dtx.t.rearrange("b (q r) -> (b q) r", q=8)),
                         (B_q.t[:], scr_B.t.rearrange("b (q r) -> (b q) r", q=8)),
                         (C_q.t[:], scr_C.t.rearrange("b (q r) -> (b q) r", q=8))], d_q,
                  R=[scr_dec, scr_dtx, scr_B, scr_C], W=[dec_q, dtx_q, B_q, C_q])
            hq_b = [sb("hq%d" % j, [128, 1024], F32, ph) for j in range(2)]
            ob_b = [sb("ob%d" % j, [128, 1024], F32, ph) for j in range(2)]
            hq_d = [P.dsem() for _ in range(2)]
            ob_d = [P.dsem() for _ in range(2)]
            phl = ph.enter_context(ExitStack())
            acc_b = [Buf(arena[:, 0:2048]), Buf(arena[:, 2048:4096])]
            gc_b = [Buf(arena[:, 4096:6144])]
            u_b = [Buf(arena[:, 6144:8194]), Buf(arena[:, 8194:10244])]
            for j in range(2):
                P.op("pool", lambda e, j=j: e.memset(u_b[j].t[:, 0:2], 0.0), W=[u_b[j]])

            def sample_state_piece(i):
                hq, ob = hq_b[i % 2], ob_b[i % 2]
                cols = slice(i * 1024, (i + 1) * 1024)
                hh = i // 8
                r0 = i * 8
                if i == 0:
                    P.dma("sp", [(hq.t[:], st_ssm[:, cols])], hq_d[0], W=[hq])
                if i + 1 < 16:
                    hqn = hq_b[(i + 1) % 2]
                    P.dma("sp", [(hqn.t[:], st_ssm[:, (i + 1) * 1024:(i + 2) * 1024])], hq_d[(i + 1) % 2], W=[hqn])

                def outer(e):
                    last = None
                    for r in range(8):
                        last = e.activation(out=ob.t[:, r * 128:(r + 1) * 128], in_=B_q.t[:], func=AF.Copy,
                                            scale=dtx_q.t[:, r0 + r:r0 + r + 1])
                    return last
                P.op("act", outer, R=[dtx_q, B_q], W=[ob])
                P.op("dve", lambda e: e.scalar_tensor_tensor(out=ob.t[:], in0=hq.t[:], scalar=dec_q.t[:, hh:hh + 1],
                                                             in1=ob.t[:], op0=ALU.mult, op1=ALU.add),
                     R=[hq, dec_q, ob], W=[ob])
                P.dma("sp", [(ssm_s[:, cols], ob.t[:])], ob_d[i % 2], R=[ob])
                P.op("dve", lambda e: e.tensor_tensor(
                    out=hq.t[:].rearrange("p (r n) -> p r n", r=8), in0=ob.t[:].rearrange("p (r n) -> p r n", r=8),
                    in1=C_q.t[:].unsqueeze(1).to_broadcast([128, 8, 128]), op=ALU.mult),
                    R=[ob, C_q], W=[hq])
                P.op("dve", lambda e: e.tensor_reduce(out=y_q.t[:, r0:r0 + 8],
                                                      in_=hq.t[:].rearrange("p (r n) -> p r n", r=8),
                                                      axis=AX.X, op=ALU.add), R=[hq], W=[y_q])

            def sample_y_tail():
                P.dma("sp", [(scr_y.t, y_q.t[:])], d_scr, R=[y_q], W=[scr_y])
                ys_tok = sb("ys_tok", [NS, 1024], F32, ph)
                ystmp = Alias(hq_b[0], hq_b[0].t[0:NS, 0:1024])
                yns = Alias(ob_b[0], ob_b[0].t[0:NS, 0:512].bitcast(BF16))
                ss3 = sb("ss3", [NS, 2], F32, ph)
                rs3 = sb("rs3", [NS, 2], F32, ph)
                P.dma("sp", [(ys_tok.t[:], scr_y.t.rearrange("(b q) r -> b (q r)", q=8))], d_q, R=[scr_y], W=[ys_tok])
                P.op("dve", lambda e: e.tensor_tensor(out=ystmp.t[:], in0=xs_tok.t[:], in1=c_d.t[0:NS, :], op=ALU.mult),
                     R=[xs_tok, c_d], W=[ystmp])
                P.op("dve", lambda e: e.tensor_tensor(out=ys_tok.t[:], in0=ys_tok.t[:], in1=ystmp.t[:], op=ALU.add),
                     R=[ys_tok, ystmp], W=[ys_tok])
                P.op("dve", lambda e: e.tensor_tensor(out=ys_tok.t[:], in0=ys_tok.t[:], in1=us_tok.t[:], op=ALU.mult),
                     R=[ys_tok, us_tok], W=[ys_tok])
                for g in range(2):
                    P.op("act", lambda e, g=g: e.activation(out=yns.t[:, g * 512:(g + 1) * 512],
                                                            in_=ys_tok.t[:, g * 512:(g + 1) * 512], func=AF.Square,
                                                            accum_out=ss3.t[:, g:g + 1]), R=[ys_tok], W=[yns, ss3])
                P.op("pool", lambda e: e.tensor_scalar(out=rs3.t[:], in0=ss3.t[:], scalar1=1.0 / 512, scalar2=EPS,
                                                       op0=ALU.mult, op1=ALU.add), R=[ss3], W=[rs3])
                P.op("pool", lambda e: e.tensor_tensor(out=rs3.t[:], in0=rs3.t[:], in1=mhalf.t[0:NS, 0:2], op=ALU.pow),
                     R=[rs3, mhalf], W=[rs3])
                for g in range(2):
                    P.op("dve", lambda e, g=g: e.scalar_tensor_tensor(
                        out=yns.t[:, g * 512:(g + 1) * 512], in0=ys_tok.t[:, g * 512:(g + 1) * 512],
                        scalar=rs3.t[:, g:g + 1], in1=c_nssd.t[0:NS, g * 512:(g + 1) * 512], op0=ALU.mult, op1=ALU.mult),
                        R=[ys_tok, rs3, c_nssd], W=[yns])
                pvs = pY.t[:].bitcast(BF16)

                def trys(e):
                    last = None
                    for j in range(8):
                        last = e.transpose(pvs[:, j * NS:(j + 1) * NS], yns.t[0:NS, j * 128:(j + 1) * 128],
                                           ident_b.t[0:NS, 0:NS])
                    return last
                P.op("pe", trys, R=[yns, ident_b], W=[pY])
                P.op("act", lambda e: e.activation(out=ycat_ssd.t[:, :, T:TT],
                                                   in_=pvs[:, 0:8 * NS].rearrange("p (j t) -> p j t", j=8), func=AF.Copy),
                     R=[pY], W=[ycat_ssd])

            def sc0(m):
                gc, u = gc_b[0], u_b[m % 2]
                wt = load_wtile(w_in_t[12 + 3 * m])

                def gc_consume(nb, pb, c0, n):
                    if nb < 4:
                        P.op("act", lambda e: e.activation(out=gc.t[:, c0:c0 + n], in_=pb.t[:, 0:n], func=AF.Copy),
                             R=[pb], W=[gc])
                    else:
                        P.op("act", lambda e: e.activation(out=sgc.t[:, m, :], in_=pb.t[:, 0:NS], func=AF.Copy),
                             R=[pb], W=[sgc])
                fm_matmul(wt, xnT, xnT_b, banks, bstate, gc_consume)
                wt = load_wtile(w_in_t[12 + 3 * m + 1])

                def h_consume(nb, pb, c0, n):
                    if nb < 4:
                        P.op("dve", lambda e: e.tensor_tensor(out=u.t[:, 2 + c0:2 + c0 + n], in0=pb.t[:, 0:n],
                                                              in1=gc.t[:, c0:c0 + n], op=ALU.mult),
                             R=[pb, gc], W=[u])
                    else:
                        P.op("dve", lambda e: e.tensor_tensor(out=raws_s.t[:, m, 2, :], in0=pb.t[:, 0:NS],
                                                              in1=sgc.t[:, m, :], op=ALU.mult),
                             R=[pb, sgc], W=[raws_s])
                fm_matmul(wt, xnT, xnT_b, banks, bstate, h_consume)

            def sc1(m):
                u, acc = u_b[m % 2], acc_b[m % 2]
                P.op("act", lambda e: e.activation(out=acc.t[:], in_=u.t[:, 0:T], func=AF.Identity,
                                                   scale=c_cws.t[:, m, 0:1]), R=[u, c_cws], W=[acc])
                for k in range(1, 3):
                    P.op("dve", lambda e, k=k: e.scalar_tensor_tensor(
                        out=acc.t[:], in0=u.t[:, k:k + T], scalar=c_cws.t[:, m, k:k + 1], in1=acc.t[:],
                        op0=ALU.mult, op1=ALU.add), R=[u, acc, c_cws], W=[acc])
                pst = nbank()
                P.op("pe", lambda e: e.transpose(pst.t[0:2, 0:128], u.t[:, T:T + 2], ident_f.t[:]),
                     R=[u, ident_f], W=[pst])
                stc = stc_b[m % 2]
                P.op("act", lambda e: e.activation(out=stc.t[:], in_=pst.t[0:2, 0:128], func=AF.Copy),
                     R=[pst], W=[stc])
                P.dma("sp", [(sc_p[:, m * 128:(m + 1) * 128], stc.t[:])], stc_d[m % 2], R=[stc])

            def sc2(m):
                acc = acc_b[m % 2]
                wt = load_wtile(w_in_t[12 + 3 * m + 2])

                def gb_consume(nb, pb, c0, n):
                    if nb < 4:
                        P.op("dve", lambda e: e.tensor_tensor(out=ycat_sc.t[:, m, c0:c0 + n], in0=pb.t[:, 0:n],
                                                              in1=acc.t[:, c0:c0 + n], op=ALU.mult),
                             R=[pb, acc], W=[ycat_sc])
                    else:
                        P.op("act", lambda e: e.activation(out=sgb.t[:, m, :], in_=pb.t[:, 0:NS], func=AF.Copy),
                             R=[pb], W=[sgb])
                fm_matmul(wt, xnT, xnT_b, banks, bstate, gb_consume)
                sched = {0: [0, 1, 2], 1: [3, 4, 5], 2: [6, 7], 3: [8, 9], 4: [10, 11], 5: [12, 13], 6: [14, 15]}
                for i in sched.get(m, []):
                    sample_state_piece(i)
                if m == 6:
                    sample_y_tail()
            run_pipeline(8, [sc0, sc1, sc2])
            for q in range(4):
                P.dma("act", [(wo_v[:, 4 * q:4 * q + 4, :], wo_bf[:, 4 * q:4 * q + 4, :])], P.dsem(), R=[wo_bf_b],
                      W=[wo_b[q]] + ([acc_b[0], acc_b[1], gc_b[0], u_b[0]] if q == 0 else []))
            stage_scs = Alias(u_b[1], u_b[1].t[0:NS, 0:1024])

            sacc = sb("sacc2", [128, 8, NS], F32, ph)
            stmp = sb("stmp2", [128, 8, NS], F32, ph)
            for k in range(3):
                wk = c_cws.t[:, :, k:k + 1].to_broadcast([128, 8, NS])
                if k == 0:
                    P.op("dve", lambda e, wk=wk: e.tensor_tensor(out=sacc.t[:], in0=raws_s.t[:, :, 0, :], in1=wk,
                                                                 op=ALU.mult), R=[raws_s, c_cws], W=[sacc])
                else:
                    P.op("dve", lambda e, wk=wk, k=k: e.tensor_tensor(out=stmp.t[:], in0=raws_s.t[:, :, k, :], in1=wk,
                                                                      op=ALU.mult), R=[raws_s, c_cws], W=[stmp])
                    P.op("dve", lambda e: e.tensor_tensor(out=sacc.t[:], in0=sacc.t[:], in1=stmp.t[:], op=ALU.add),
                         R=[sacc, stmp], W=[sacc])
            P.op("dve", lambda e: e.tensor_tensor(out=ycat_sc.t[:, :, T:TT], in0=sacc.t[:], in1=sgb.t[:], op=ALU.mult),
                 R=[sacc, sgb], W=[ycat_sc])
            for half in range(2):
                pb = nbank()

                def tru(e, pb=pb, half=half):
                    last = None
                    for jj in range(4):
                        last = e.transpose(pb.t[0:NS, jj * 128:(jj + 1) * 128], raws_s.t[:, half * 4 + jj, 2, :],
                                           ident_f.t[:])
                    return last
                P.op("pe", tru, R=[raws_s, ident_f], W=[pb])
                P.op("act", lambda e, pb=pb, half=half: e.activation(out=stage_scs.t[:, half * 512:(half + 1) * 512],
                                                                     in_=pb.t[0:NS, :], func=AF.Copy),
                     R=[pb], W=[stage_scs])
            P.dma("sp", [(sc_s[:, 1, :], stage_scs.t[:])], d_out, R=[stage_scs])

            P.barrier()
        S2.close()
        for nm, b in (("ycat_sc", ycat_sc), ("ycat_ssd2", ycat_ssd)):
            if nm in tapo:
                P.dma("sp", [(tapo[nm], b.t[:])], d_out, R=[b])
        if stop_after <= 4:
            P.finish()
            return nc

        SR = es.enter_context(ExitStack())
        xn2T = SR.enter_context(nc.sbuf_tensor("xn2T", [128, 8, TT], BF16, side="right"))
        xn2T_b = [Buf(xn2T) for _ in range(5)]
        x1_rows = [Buf(x1_d[i * 128:(i + 1) * 128, :]) for i in range(NCH)] + [Buf(x1_d[T:TT, :])]
        with ExitStack() as ph:
            pb4 = [ps("p4b%d" % j, [128, 512], F32, ph) for j in range(4)]
            pT = [ps("p4T%d" % j, [128, 512], F32, ph) for j in range(2)]
            c_nffn = sb("c_nffn", [128, 1024], F32, ph)
            P.dma("sp", [(c_nffn.t[:], nffn_bc)], d_cn, W=[c_nffn])
            xt_b = [sb("p4x%d" % j, [128, 1024], F32, ph) for j in range(3)]
            xt_d = [P.dsem() for _ in range(3)]
            x1_b = [sb("p4y%d" % j, [128, 1024], F32, ph) for j in range(3)]
            x1_dd = [P.dsem() for _ in range(3)]
            xnb_b = [sb("p4n%d" % j, [128, 1024], BF16, ph) for j in range(2)]
            ss_b = [sb("p4s%d" % j, [128, 1], F32, ph) for j in range(2)]
            rs_b = [sb("p4r%d" % j, [128, 1], F32, ph) for j in range(2)]
            def o0(i):
                n = 128 if i < NCH else NS
                c0 = i * 128
                src = xp[c0:c0 + 128, :] if i < NCH else xs
                xt, x1t = xt_b[i % 3], x1_b[i % 3]
                P.dma("sp", [(xt.t[0:n, :], src)], xt_d[i % 3], W=[xt])
                for hf in range(2):
                    pb = pb4[(2 * i + hf) % 4]

                    def mo(e, pb=pb, hf=hf):
                        last = None
                        for k in range(16):
                            lhsT = ycat_ssd.t[:, k, c0:c0 + n] if k < 8 else ycat_sc.t[:, k - 8, c0:c0 + n]
                            last = e.matmul(pb.t[0:n, :], lhsT=lhsT, rhs=wo_v[:, k, hf * 512:(hf + 1) * 512],
                                            start=(k == 0), stop=(k == 15))
                        return last
                    P.op("pe", mo, R=[ycat_ssd, ycat_sc] + wo_b, W=[pb])
                    P.op("dve", lambda e, pb=pb, hf=hf: e.tensor_tensor(
                        out=x1t.t[0:n, hf * 512:(hf + 1) * 512], in0=pb.t[0:n, :], in1=xt.t[0:n, hf * 512:(hf + 1) * 512],
                        op=ALU.add), R=[pb, xt], W=[x1t])
                P.dma("sp", [(x1_rows[i].t, x1t.t[0:n, :])], x1_dd[i % 3], R=[x1t], W=[x1_rows[i]])

            def o1(i):
                n = 128 if i < NCH else NS
                rms_scale_transpose(i, x1_b[i % 3], n, c_nffn, ss_b[i % 2], rs_b[i % 2], xnb_b[i % 2], None, None, None,
                                    do_tr=False)

            def o2(i):
                n = 128 if i < NCH else NS
                transpose_to(i, n, xnb_b[i % 2], pT[i % 2], xn2T, xn2T_b)
            run_pipeline(NCH + 1, [o0, o1, o2])
            P.barrier()
        S1b.close()
        S1.close()
        if "xn2T" in tapo:
            P.dma("sp", [(tapo["xn2T"], xn2T[:])], d_out, R=xn2T_b)
        if stop_after <= 5:
            P.finish()
            return nc

        S4 = es.enter_context(ExitStack())
        hff = sb("hff", [128, NFT, TT], BF16, S4)
        arena5 = S4.enter_context(nc.sbuf_tensor("arena5", [128, 11264], F32))
        wd_v = arena5[:, :].bitcast(BF16).rearrange("p (k c) -> p k c", k=NFT)
        with ExitStack() as ph:
            banks = [ps("p5b%d" % j, [128, 512], F32, ph) for j in range(8)]
            bstate = {"n": 0}

            def nbank():
                b = banks[bstate["n"] % len(banks)]
                bstate["n"] += 1
                return b
            load_wtile = make_wring(ph, "c")
            raw_b = [Buf(arena5[:, 0:2050]), Buf(arena5[:, 2050:4100])]
            acc_b = [Buf(arena5[:, 4100:6148]), Buf(arena5[:, 6148:8196])]
            sg_b = [Buf(arena5[:, 8196:9220].bitcast(BF16)), Buf(arena5[:, 9220:10244].bitcast(BF16))]
            for j in range(2):
                P.op("pool", lambda e, j=j: e.memset(raw_b[j].t[:, 0:2], 0.0), W=[raw_b[j]])
            sup = sb("sup", [128, NFT, NS], F32, ph)
            stg_b = [sb("p5st%d" % j, [2, 128], F32, ph) for j in range(2)]
            stg_d = [P.dsem() for _ in range(2)]
            def f0(m):
                raw = raw_b[m % 2]
                wt = load_wtile(w_ffn_t[2 * m])

                def g_consume(nb, pb, c0, n):
                    if nb < 4:
                        P.op("act", lambda e: e.activation(out=raw.t[:, 2 + c0:2 + c0 + n], in_=pb.t[:, 0:n],
                                                           func=AF.Copy), R=[pb], W=[raw])
                    else:
                        P.op("act", lambda e: e.activation(out=raws_f.t[:, m, 2, :], in_=pb.t[:, 0:NS], func=AF.Copy),
                             R=[pb], W=[raws_f])
                fm_matmul(wt, xn2T, xn2T_b, banks, bstate, g_consume)

            def f1(m):
                raw, acc = raw_b[m % 2], acc_b[m % 2]
                P.op("act", lambda e: e.activation(
                    out=acc.t[:], in_=raw.t[:, 0:T], func=AF.Identity, bias=c_cbf.t[:, m:m + 1],
                    scale=c_cwf.t[:, m, 0:1]), R=[raw, c_cwf, c_cbf], W=[acc])
                for k in range(1, 3):
                    P.op("dve", lambda e, k=k: e.scalar_tensor_tensor(
                        out=acc.t[:], in0=raw.t[:, k:k + T], scalar=c_cwf.t[:, m, k:k + 1], in1=acc.t[:],
                        op0=ALU.mult, op1=ALU.add), R=[raw, acc, c_cwf], W=[acc])
                pst = nbank()
                stg = stg_b[m % 2]
                P.op("pe", lambda e: e.transpose(pst.t[0:2, 0:128], raw.t[:, T:T + 2], ident_f.t[:]),
                     R=[raw, ident_f], W=[pst])
                P.op("act", lambda e: e.activation(out=stg.t[:], in_=pst.t[0:2, 0:128], func=AF.Copy),
                     R=[pst], W=[stg])
                P.dma("sp", [(ffn_p[:, m * 128:(m + 1) * 128], stg.t[:])], stg_d[m % 2], R=[stg])

            def f2(m):
                acc, sg = acc_b[m % 2], sg_b[m % 2]
                P.op("act", lambda e: e.activation(out=sg.t[:], in_=acc.t[:], func=AF.Silu), R=[acc], W=[sg])
                wt = load_wtile(w_ffn_t[2 * m + 1])

                def u_consume(nb, pb, c0, n):
                    if nb < 4:
                        P.op("dve", lambda e: e.tensor_tensor(out=hff.t[:, m, c0:c0 + n], in0=pb.t[:, 0:n],
                                                              in1=sg.t[:, c0:c0 + n], op=ALU.mult),
                             R=[pb, sg], W=[hff])
                    else:
                        P.op("act", lambda e: e.activation(out=sup.t[:, m, :], in_=pb.t[:, 0:NS], func=AF.Copy),
                             R=[pb], W=[sup])
                fm_matmul(wt, xn2T, xn2T_b, banks, bstate, u_consume)
            run_pipeline(NFT, [f0, f1, f2])
            wd_b = []
            for qi, (k0, k1) in enumerate(((0, 4), (4, 8), (8, 12), (12, 16), (16, 20), (20, 22))):
                wd_b.append(Buf(wd_v))
                P.dma("act", [(wd_v[:, k0:k1, :], wd_bf[:, k0:k1, :])], P.dsem(), R=[wd_bf_b],
                      W=[wd_b[-1]] + (raw_b + acc_b + sg_b if qi == 0 else []))
            sacc = sb("sacc3", [128, NFT, NS], F32, ph)
            stmp = sb("stmp3", [128, NFT, NS], F32, ph)
            for k in range(3):
                wk = c_cwf.t[:, :, k:k + 1].to_broadcast([128, NFT, NS])
                if k == 0:
                    P.op("dve", lambda e, wk=wk: e.tensor_tensor(out=sacc.t[:], in0=raws_f.t[:, :, 0, :], in1=wk,
                                                                 op=ALU.mult), R=[raws_f, c_cwf], W=[sacc])
                else:
                    P.op("dve", lambda e, wk=wk, k=k: e.tensor_tensor(out=stmp.t[:], in0=raws_f.t[:, :, k, :], in1=wk,
                                                                      op=ALU.mult), R=[raws_f, c_cwf], W=[stmp])
                    P.op("dve", lambda e: e.tensor_tensor(out=sacc.t[:], in0=sacc.t[:], in1=stmp.t[:], op=ALU.add),
                         R=[sacc, stmp], W=[sacc])
            P.op("dve", lambda e: e.tensor_tensor(out=sacc.t[:], in0=sacc.t[:],
                                                  in1=c_cbf.t[:].unsqueeze(2).to_broadcast([128, NFT, NS]),
                                                  op=ALU.add), R=[sacc, c_cbf], W=[sacc])
            P.op("act", lambda e: e.activation(out=stmp.t[:], in_=sacc.t[:], func=AF.Silu), R=[sacc], W=[stmp])
            P.op("dve", lambda e: e.tensor_tensor(out=hff.t[:, :, T:TT], in0=stmp.t[:], in1=sup.t[:], op=ALU.mult),
                 R=[stmp, sup], W=[hff])
            stg2_b = [sb("p5sf%d" % j, [NS, 512], F32, ph) for j in range(2)]
            stg2_d = [P.dsem() for _ in range(2)]
            for q in range(6):
                nt = 4 if q < 5 else 2
                pb = nbank()
                stg = stg2_b[q % 2]

                def trf(e, pb=pb, q=q, nt=nt):
                    last = None
                    for jj in range(nt):
                        last = e.transpose(pb.t[0:NS, jj * 128:(jj + 1) * 128], raws_f.t[:, q * 4 + jj, 2, :],
                                           ident_f.t[:])
                    return last
                P.op("pe", trf, R=[raws_f, ident_f], W=[pb])
                P.op("act", lambda e, pb=pb, stg=stg, nt=nt: e.activation(out=stg.t[:, 0:nt * 128],
                                                                          in_=pb.t[0:NS, 0:nt * 128], func=AF.Copy),
                     R=[pb], W=[stg])
                P.dma("sp", [(ffn_s[:, 1, q * 512:q * 512 + nt * 128], stg.t[:, 0:nt * 128])], stg2_d[q % 2], R=[stg])
            P.barrier()
        SR.close()
        if stop_after <= 6:
            P.finish()
            return nc

        with ExitStack() as ph:
            pb6 = [ps("p6b%d" % j, [128, 512], F32, ph) for j in range(4)]
            c_nfin = sb("c_nfin", [128, 1024], F32, ph)
            P.dma("sp", [(c_nfin.t[:], nfin_bc)], d_cn, W=[c_nfin])
            xt_b = [sb("p6x%d" % j, [128, 1024], F32, ph) for j in range(3)]
            xt_d = [P.dsem() for _ in range(3)]
            x2_b = [sb("p6y%d" % j, [128, 1024], F32, ph) for j in range(2)]
            yo_b = [sb("p6o%d" % j, [128, 1024], F32, ph) for j in range(2)]
            yo_d = [P.dsem() for _ in range(2)]
            jk_b = [sb("p6j%d" % j, [128, 1024], BF16, ph) for j in range(2)]
            ss_b = [sb("p6s%d" % j, [128, 1], F32, ph) for j in range(2)]
            rs_b = [sb("p6r%d" % j, [128, 1], F32, ph) for j in range(2)]
            for i in range(NCH + 1):
                n = 128 if i < NCH else NS
                c0 = i * 128
                xt, x2t, yo = xt_b[i % 3], x2_b[i % 2], yo_b[i % 2]
                P.dma("sp", [(xt.t[0:n, :], x1_rows[i].t)], xt_d[i % 3], R=[x1_rows[i]], W=[xt])
                for hf in range(2):
                    pb = pb6[(2 * i + hf) % 4]

                    def md(e, pb=pb, hf=hf, c0=c0, n=n):
                        last = None
                        for k in range(NFT):
                            last = e.matmul(pb.t[0:n, :], lhsT=hff.t[:, k, c0:c0 + n],
                                            rhs=wd_v[:, k, hf * 512:(hf + 1) * 512], start=(k == 0), stop=(k == NFT - 1))
                        return last
                    P.op("pe", md, R=[hff] + wd_b, W=[pb])
                    P.op("dve", lambda e, pb=pb, hf=hf, n=n, xt=xt, x2t=x2t: e.tensor_tensor(
                        out=x2t.t[0:n, hf * 512:(hf + 1) * 512], in0=pb.t[0:n, :], in1=xt.t[0:n, hf * 512:(hf + 1) * 512],
                        op=ALU.add), R=[pb, xt], W=[x2t])
                rms_scale_transpose(i, x2t, n, c_nfin, ss_b[i % 2], rs_b[i % 2], jk_b[i % 2], None, None, None, act_out=yo)
                dst = y_p[c0:c0 + 128, :] if i < NCH else y_s
                P.dma("sp", [(dst, yo.t[0:n, :])], yo_d[i % 2], R=[yo])
            P.barrier()
        P.finish()
    return nc


def _prep_shared(inp):
    f = np.float32
    w_in = np.asarray(inp["w_in"][0], f)
    cols = [O_XBC + 128 * m for m in range(12)]
    for m in range(8):
        cols += [O_GC + 128 * m, O_H + 128 * m, O_GB + 128 * m]

    def ftile(w, c0, n=128):
        return np.ascontiguousarray(w[:, c0:c0 + n].reshape(8, 128, n).transpose(1, 0, 2))
    sh = {}
    sh["w_in_t"] = np.stack([ftile(w_in, c) for c in cols])
    sh["w_dt"] = ftile(w_in, O_DT, 16)
    sh["w_z"] = ftile(w_in, O_Z, 1024)
    sh["w_out_t"] = np.ascontiguousarray(np.asarray(inp["w_out"][0], f).reshape(16, 128, 1024).transpose(1, 0, 2))
    w_ffn = np.asarray(inp["w_ffn_in"][0], f)
    fcols = []
    for m in range(NFT):
        fcols += [128 * m, DFF + 128 * m]
    sh["w_ffn_t"] = np.stack([ftile(w_ffn, c) for c in fcols])
    sh["w_down_t"] = np.ascontiguousarray(np.asarray(inp["w_down"][0], f).reshape(NFT, 128, 1024).transpose(1, 0, 2))

    def bc(v):
        return np.ascontiguousarray(np.broadcast_to(np.asarray(v, f).reshape(1, -1), (128, np.asarray(v).size)))
    sh["nmix_bc"] = bc(inp["norm_mix_w"][0])
    sh["nffn_bc"] = bc(inp["norm_ffn_w"][0])
    sh["nfin_bc"] = bc(inp["norm_final_w"])
    sh["nssd_bc"] = bc(inp["ssd_norm_w"][0])
    sh["d_bc"] = bc(np.repeat(np.asarray(inp["ssd_d"][0], f), 64))
    sh["alog_bc"] = bc(inp["ssd_a_log"][0])
    sh["dtb"] = bc(inp["ssd_dt_bias"][0])

    def fmaj(v, nt):
        v = np.asarray(v, f)
        lead = v.shape[:-1]
        v = v.reshape(lead + (nt, 128))
        v = np.moveaxis(v, -1, 0)
        v = np.moveaxis(v, -1, 1)
        return np.ascontiguousarray(v)
    sh["cw_xbc"] = fmaj(inp["ssd_conv_w"][0], 12)
    sh["cb_xbc"] = fmaj(inp["ssd_conv_b"][0], 12)
    sh["cw_sc"] = fmaj(inp["sc_conv_w"][0], 8)
    sh["cw_ffn"] = fmaj(inp["ffn_conv_w"][0], NFT)
    sh["cb_ffn"] = fmaj(inp["ffn_conv_b"][0], NFT)
    sh["_fmaj"] = fmaj
    return sh


def _in_maps(inp):
    f = np.float32
    sh = _prep_shared(inp)
    fmaj = sh.pop("_fmaj")
    maps = []

    def padr(a):
        a = np.transpose(a, (0, 1, 3, 2))
        z = np.zeros(a.shape[:2] + (1, a.shape[3]), a.dtype)
        return np.ascontiguousarray(np.concatenate([a, z], axis=2))
    for c in range(8):
        bs = slice(NS * c, NS * (c + 1))
        d = dict(sh)
        d["xp"] = np.ascontiguousarray(inp["x_prompt"][c], f)
        d["xs"] = np.ascontiguousarray(inp["x_sample"][bs, 0, :], f)
        d["st_ssm"] = np.ascontiguousarray(np.asarray(inp["state_ssm"][0, bs], f).reshape(128, 16384))
        sx = np.asarray(inp["state_ssd_conv"][0, bs], f)
        d["st_xbc_n"] = np.ascontiguousarray(sx)
        d["st_xbc_f"] = padr(fmaj(sx, 12))
        ss = np.asarray(inp["state_short_conv"][0, bs], f)
        d["st_sc_n"] = np.ascontiguousarray(ss)
        d["st_sc_f"] = padr(fmaj(ss, 8))
        sf = np.asarray(inp["state_ffn_conv"][0, bs], f)
        d["st_ffn_n"] = np.ascontiguousarray(sf)
        d["st_ffn_f"] = padr(fmaj(sf, NFT))
        maps.append(d)
    return maps


def kernel(**inp):
    nc = build()
    maps = _in_maps(inp)
    res = run_bass_kernel_spmd(nc, maps, core_ids=list(range(8)))
    R = res.results
    f = np.float32
    y_prompt = np.stack([R[c]["y_p"] for c in range(8)]).astype(f)
    y_sample = np.concatenate([R[c]["y_s"] for c in range(8)]).reshape(128, 1, 1024).astype(f)
    ssm_p = np.stack([R[c]["ssm_p"].reshape(16, 64, 128) for c in range(8)])[None].astype(f)
    xbc_p = np.stack([R[c]["xbc_p"] for c in range(8)])[None].astype(f)
    sc_p = np.stack([R[c]["sc_p"] for c in range(8)])[None].astype(f)
    ffn_p = np.stack([R[c]["ffn_p"] for c in range(8)])[None].astype(f)
    ssm_s = np.concatenate([R[c]["ssm_s"].reshape(16, 16, 64, 128) for c in range(8)])[None].astype(f)
    xbc_s = np.concatenate([R[c]["xbc_s"] for c in range(8)])[None].astype(f)
    sc_s = np.concatenate([R[c]["sc_s"] for c in range(8)])[None].astype(f)
    ffn_s = np.concatenate([R[c]["ffn_s"] for c in range(8)])[None].astype(f)
    return (y_prompt, y_sample, ssm_p, xbc_p, sc_p, ffn_p, ssm_s, xbc_s, sc_s, ffn_s)
```
